# Optimizing a Trainium2 kernel written in Bass

```python
import jax, jax.numpy as jnp
from jax import lax
import numpy as np

D_MODEL = 1024
BATCH = 2
SEQ = 16384
DEPTH = 2
DEC_BATCH = 32
DEC_SEQ = 2048
PAST_LEN = 128

HEAD_DIM = 64
A_HEADS = 8
A_WIDTH = A_HEADS * HEAD_DIM
A_PATTERNS = ((128, 1), (512, 4), (2048, 16))
ROPE_THETA = 500000.0
ROPE_DIM = HEAD_DIM // 4
B_HEADS = 4
B_KDIM = 32
B_VDIM = 64
B_QK = B_HEADS * B_KDIM
B_WIDTH = B_HEADS * B_VDIM
B_GATE_RANK = 16
B_GATE_TAU = 16.0
B_CHUNK = 64
C_HEADS = 4
C_DIM = 64
C_WIDTH = C_HEADS * C_DIM
C_CHUNK = 128
RET_THETA = 10000.0
IN_SPLITS = (A_WIDTH, A_WIDTH, A_WIDTH,
             B_QK, B_QK, B_WIDTH, B_WIDTH, 2 * B_GATE_RANK,
             C_WIDTH, C_WIDTH, C_WIDTH, C_WIDTH)
N_IN = 3 * A_WIDTH + 2 * B_QK + 2 * B_WIDTH + 2 * B_GATE_RANK + 4 * C_WIDTH
MIX_WIDTH = A_WIDTH + B_WIDTH + C_WIDTH
D_FF = 4 * D_MODEL
PLE_DIM = 256
EPS = 1e-6
NEG = -1e30

kernel_name = 'hybrid_bidir_dilated_gla_retention_encoder'


def rmsnorm(x, g):
    xf = x.astype(jnp.float32)
    y = xf * lax.rsqrt(jnp.mean(xf * xf, axis=-1, keepdims=True) + EPS)
    return (y * g.astype(jnp.float32)).astype(x.dtype)


def head_rmsnorm(x, g):
    b, s, h, d = x.shape
    return rmsnorm(x, g.reshape(h, d)).reshape(b, s, h * d)


def rotary(x, pos, rot_dim, theta):
    half = rot_dim // 2
    inv_freq = 1.0 / (theta ** (jnp.arange(half, dtype=jnp.float32) * (2.0 / rot_dim)))
    ang = pos[:, None] * inv_freq[None, :]
    cos = jnp.cos(ang)[None, :, None, :]
    sin = jnp.sin(ang)[None, :, None, :]
    x1 = x[..., :half]
    x2 = x[..., half:rot_dim]
    return jnp.concatenate([x1 * cos - x2 * sin, x2 * cos + x1 * sin, x[..., rot_dim:]], axis=-1)


def banded_attention(q, k, v, radius):
    n, L, h, d = q.shape
    w = radius
    nb = -(-L // w)
    lp = nb * w
    qb = jnp.pad(q, ((0, 0), (0, lp - L), (0, 0), (0, 0))).reshape(n, nb, w, h, d)

    def windows(t):
        tb = jnp.pad(t, ((0, 0), (w, lp - L + w), (0, 0), (0, 0))).reshape(n, nb + 2, w, h, d)
        return jnp.concatenate([tb[:, :-2], tb[:, 1:-1], tb[:, 2:]], axis=2)

    kw = windows(k)
    vw = windows(v)
    qpos = jnp.arange(nb)[:, None] * w + jnp.arange(w)[None, :]
    kpos = jnp.arange(nb)[:, None] * w - w + jnp.arange(3 * w)[None, :]
    rel = qpos[:, :, None] - kpos[:, None, :]
    mask = (jnp.abs(rel) <= radius) & (kpos[:, None, :] >= 0) & (kpos[:, None, :] < L)
    s = jnp.einsum('nbqhd,nbkhd->nbhqk', qb, kw)
    s = jnp.where(mask[None, :, None], s, NEG)
    m = jnp.max(s, axis=-1, keepdims=True)
    p = jnp.exp(s - m)
    den = jnp.sum(p, axis=-1, keepdims=True)
    o = jnp.einsum('nbhqk,nbkhd->nbqhd', p, vw) / jnp.swapaxes(den, 2, 3)
    lse = jnp.swapaxes((m + jnp.log(den))[..., 0], 2, 3)
    return o.reshape(n, lp, h, d)[:, :L], lse.reshape(n, lp, h)[:, :L]


def dilated_attention(q, k, v):
    b, s, h, d = q.shape
    outs = []
    lses = []
    for window, dil in A_PATTERNS:
        radius = window // (2 * dil)
        n_sub = s // dil

        def to_res(t):
            return t.reshape(b, n_sub, dil, h, d).transpose(0, 2, 1, 3, 4).reshape(b * dil, n_sub, h, d)

        o, lse = banded_attention(to_res(q), to_res(k), to_res(v), radius)
        outs.append(o.reshape(b, dil, n_sub, h, d).transpose(0, 2, 1, 3, 4).reshape(b, s, h, d))
        lses.append(lse.reshape(b, dil, n_sub, h).transpose(0, 2, 1, 3).reshape(b, s, h))
    wts = jax.nn.softmax(jnp.stack(lses), axis=0)
    return jnp.einsum('gbsh,gbshd->bshd', wts, jnp.stack(outs))


def gla_chunked(q, k, v, log_a, strict):
    bn, s, h, kd = q.shape
    vd = v.shape[-1]
    c = B_CHUNK
    nc = s // c

    def chunks(t):
        return t.reshape(bn, nc, c, h, t.shape[-1]).transpose(1, 0, 3, 2, 4)

    qc, kc, vc = chunks(q), chunks(k), chunks(v)
    bc = jnp.cumsum(chunks(log_a), axis=3)
    idx = jnp.arange(c)
    mask = (idx[None, :] < idx[:, None]) if strict else (idx[None, :] <= idx[:, None])

    def step(state, inp):
        qt, kt, vt, bt = inp
        inter = jnp.einsum('bhtk,bhkv->bhtv', qt * jnp.exp(bt), state)
        diff = bt[:, :, :, None, :] - bt[:, :, None, :, :]
        decay = jnp.exp(jnp.where(mask[:, :, None], diff, NEG))
        att = jnp.einsum('bhtk,bhsk,bhtsk->bhts', qt, kt, decay)
        intra = jnp.einsum('bhts,bhsv->bhtv', att, vt)
        blast = bt[:, :, -1:, :]
        new_state = jnp.exp(blast[:, :, 0, :])[..., None] * state + jnp.einsum(
            'bhsk,bhsv->bhkv', kt * jnp.exp(blast - bt), vt)
        return new_state, inter + intra

    _, o = lax.scan(step, jnp.zeros((bn, h, kd, vd), jnp.float32), (qc, kc, vc, bc))
    return o.transpose(1, 0, 3, 2, 4).reshape(bn, s, h, vd)


def retention_chunked(q, k, v, log_gamma, strict):
    bn, s, h, d = q.shape
    c = C_CHUNK
    nc = s // c
    qc = q.reshape(bn, nc, c, h, d)
    kc = k.reshape(bn, nc, c, h, d)
    vc = v.reshape(bn, nc, c, h, d)
    idx = jnp.arange(c, dtype=jnp.float32)
    rel = idx[:, None] - idx[None, :]
    mask = (rel > 0) if strict else (rel >= 0)
    dmat = jnp.where(mask[None], jnp.exp(jnp.where(mask, rel, 0.0)[None] * log_gamma[:, None, None]), 0.0)
    scores = jnp.einsum('bnthd,bnshd->bnhts', qc, kc) * dmat[None, None]
    intra = jnp.einsum('bnhts,bnshe->bnthe', scores, vc)
    kdec = jnp.exp((c - 1.0 - idx)[None, :] * log_gamma[:, None])
    chunk_kv = jnp.einsum('bnshd,bnshe,hs->nbhde', kc, vc, kdec)
    chunk_decay = jnp.exp(c * log_gamma)[None, :, None, None]

    def step(r, kv):
        return chunk_decay * r + kv, r

    _, r_prev = lax.scan(step, jnp.zeros((bn, h, d, d), jnp.float32), chunk_kv)
    qdec = jnp.exp((idx + 1.0)[None, :] * log_gamma[:, None])
    inter = jnp.einsum('bnthd,nbhde,ht->bnthe', qc, r_prev, qdec)
    return (intra + inter).reshape(bn, s, h, d)


def rev(t):
    return t[:, ::-1]


def trunk_layer(h, ple, ln_mix, w_in, attn_q_norm, attn_k_norm, gla_gate_up, gla_gate_bias,
                gla_out_norm, ret_decay_raw, ret_out_norm, w_out, ln_mlp, w_mlp_in, w_mlp_out,
                ln_pe, w_pe_gate, w_pe_proj):
    bn, s, _ = h.shape
    dt = h.dtype
    f32 = jnp.float32
    u = rmsnorm(h, ln_mix)
    z = (u @ w_in).astype(f32)
    points = []
    acc = 0
    for width in IN_SPLITS[:-1]:
        acc += width
        points.append(acc)
    aq, ak, av, bq, bk, bv, br, bg, cq, ck, cv, cg = jnp.split(z, points, axis=-1)
    pos = jnp.arange(s, dtype=f32)

    aq = rotary(rmsnorm(aq.reshape(bn, s, A_HEADS, HEAD_DIM), attn_q_norm), pos, ROPE_DIM, ROPE_THETA) * (HEAD_DIM ** -0.5)
    ak = rotary(rmsnorm(ak.reshape(bn, s, A_HEADS, HEAD_DIM), attn_k_norm), pos, ROPE_DIM, ROPE_THETA)
    av = av.reshape(bn, s, A_HEADS, HEAD_DIM)
    o_a = dilated_attention(aq, ak, av).reshape(bn, s, A_WIDTH)

    bq = bq.reshape(bn, s, B_HEADS, B_KDIM) * (B_KDIM ** -0.5)
    bk = bk.reshape(bn, s, B_HEADS, B_KDIM)
    bv = bv.reshape(bn, s, B_HEADS, B_VDIM)
    glr = bg.reshape(bn, s, 2, B_GATE_RANK)
    gate_logits = jnp.einsum('bsjr,jrk->jbsk', glr, gla_gate_up.astype(f32)) + gla_gate_bias.astype(f32)[:, None, None, :]
    log_a = (jax.nn.log_sigmoid(gate_logits) / B_GATE_TAU).reshape(2, bn, s, B_HEADS, B_KDIM)
    o_bf = gla_chunked(bq, bk, bv, log_a[0], False)
    o_bb = rev(gla_chunked(rev(bq), rev(bk), rev(bv), rev(log_a[1]), True))
    o_b = head_rmsnorm(o_bf + o_bb, gla_out_norm) * jax.nn.silu(br)

    cq = rotary(cq.reshape(bn, s, C_HEADS, C_DIM), pos, C_DIM, RET_THETA)
    ck = rotary(ck.reshape(bn, s, C_HEADS, C_DIM), pos, C_DIM, RET_THETA) * (C_DIM ** -0.5)
    cv = cv.reshape(bn, s, C_HEADS, C_DIM)
    log_gamma = jax.nn.log_sigmoid(ret_decay_raw.astype(f32))
    o_cf = retention_chunked(cq, ck, cv, log_gamma[0], False)
    o_cb = rev(retention_chunked(rev(cq), rev(ck), rev(cv), log_gamma[1], True))
    o_c = head_rmsnorm(o_cf + o_cb, ret_out_norm) * jax.nn.silu(cg)

    mix = jnp.concatenate([o_a, o_b, o_c], axis=-1).astype(dt)
    h = h + mix @ w_out

    m = rmsnorm(h, ln_mlp)
    h = h + jnp.square(jax.nn.relu(m @ w_mlp_in)) @ w_mlp_out

    gate = jax.nn.sigmoid(rmsnorm(h, ln_pe) @ w_pe_gate)
    return h + gate * (ple @ w_pe_proj)


def run_trunk(x, p, ln_mix, w_in, attn_q_norm, attn_k_norm, gla_gate_up, gla_gate_bias,
              gla_out_norm, ret_decay_raw, ret_out_norm, w_out, ln_mlp, w_mlp_in, w_mlp_out,
              ln_pe, w_pe_gate, w_pe_proj):
    h = x
    for i in range(DEPTH):
        h = trunk_layer(h, p[i], ln_mix[i], w_in[i], attn_q_norm[i], attn_k_norm[i],
                        gla_gate_up[i], gla_gate_bias[i], gla_out_norm[i], ret_decay_raw[i],
                        ret_out_norm[i], w_out[i], ln_mlp[i], w_mlp_in[i], w_mlp_out[i],
                        ln_pe[i], w_pe_gate[i], w_pe_proj[i])
    return h


def setup_inputs(seed: int = 0) -> dict:
    key = jax.random.key(seed)
    ks = jax.random.split(key, 24)
    f32 = jnp.float32

    def nrm(k, shape, scale):
        return jax.random.normal(k, shape, f32) * scale

    ret_base = jnp.log(2.0 ** (5.0 + jnp.arange(C_HEADS, dtype=f32)) - 1.0)
    return {
        'x_prompt': nrm(ks[0], (BATCH, SEQ, D_MODEL), 1.0),
        'x_sample': nrm(ks[1], (DEC_BATCH, DEC_SEQ, D_MODEL), 1.0),
        'p_prompt': nrm(ks[2], (DEPTH, BATCH, SEQ, PLE_DIM), 1.0),
        'p_sample': nrm(ks[3], (DEPTH, DEC_BATCH, DEC_SEQ, PLE_DIM), 1.0),
        'ln_mix': 1.0 + nrm(ks[4], (DEPTH, D_MODEL), 0.02),
        'w_in': nrm(ks[5], (DEPTH, D_MODEL, N_IN), D_MODEL ** -0.5),
        'attn_q_norm': 1.0 + nrm(ks[6], (DEPTH, HEAD_DIM), 0.02),
        'attn_k_norm': 1.0 + nrm(ks[7], (DEPTH, HEAD_DIM), 0.02),
        'gla_gate_up': nrm(ks[8], (DEPTH, 2, B_GATE_RANK, B_QK), B_GATE_RANK ** -0.5),
        'gla_gate_bias': nrm(ks[9], (DEPTH, 2, B_QK), 0.1),
        'gla_out_norm': 1.0 + nrm(ks[10], (DEPTH, B_WIDTH), 0.02),
        'ret_decay_raw': ret_base[None, None, :] + nrm(ks[11], (DEPTH, 2, C_HEADS), 0.1),
        'ret_out_norm': 1.0 + nrm(ks[12], (DEPTH, C_WIDTH), 0.02),
        'w_out': nrm(ks[13], (DEPTH, MIX_WIDTH, D_MODEL), MIX_WIDTH ** -0.5),
        'ln_mlp': 1.0 + nrm(ks[14], (DEPTH, D_MODEL), 0.02),
        'w_mlp_in': nrm(ks[15], (DEPTH, D_MODEL, D_FF), D_MODEL ** -0.5),
        'w_mlp_out': nrm(ks[16], (DEPTH, D_FF, D_MODEL), D_FF ** -0.5),
        'ln_pe': 1.0 + nrm(ks[17], (DEPTH, D_MODEL), 0.02),
        'w_pe_gate': nrm(ks[18], (DEPTH, D_MODEL, D_MODEL), D_MODEL ** -0.5),
        'w_pe_proj': nrm(ks[19], (DEPTH, PLE_DIM, D_MODEL), PLE_DIM ** -0.5),
    }


def reference(x_prompt, x_sample, p_prompt, p_sample, ln_mix, w_in, attn_q_norm, attn_k_norm,
              gla_gate_up, gla_gate_bias, gla_out_norm, ret_decay_raw, ret_out_norm, w_out,
              ln_mlp, w_mlp_in, w_mlp_out, ln_pe, w_pe_gate, w_pe_proj):
    y_prompt = run_trunk(x_prompt, p_prompt, ln_mix, w_in, attn_q_norm, attn_k_norm,
                         gla_gate_up, gla_gate_bias, gla_out_norm, ret_decay_raw, ret_out_norm,
                         w_out, ln_mlp, w_mlp_in, w_mlp_out, ln_pe, w_pe_gate, w_pe_proj)
    y_sample = run_trunk(x_sample, p_sample, ln_mix, w_in, attn_q_norm, attn_k_norm,
                         gla_gate_up, gla_gate_bias, gla_out_norm, ret_decay_raw, ret_out_norm,
                         w_out, ln_mlp, w_mlp_in, w_mlp_out, ln_pe, w_pe_gate, w_pe_proj)
    return (y_prompt, y_sample)
```

```python
import numpy as np
import concourse.bass as bass
import concourse.mybir as mybir
from concourse.bass_utils import run_bass_kernel_spmd

F32 = mybir.dt.float32
BF16 = mybir.dt.bfloat16
AF = mybir.ActivationFunctionType
ALU = mybir.AluOpType
AX = mybir.AxisListType

D = 1024
NIN = 3360
EPS = 1e-6
NPIECE = 21
PW = 4096
RING = 18
C_AQ, C_AK, C_AV, C_B, C_G, C_CQK, C_CVG = 0, 512, 1024, 1536, 2048, 2336, 2848
K_ID, K_TRII, K_TRIE, K_MF, K_MB, K_TMS = 0, 128, 256, 384, 512, 640
K_TP1, K_CM1, K_CT, K_S0, K_NEG16, K_ONE, K_BD = 768, 769, 770, 771, 773, 774, 776
NCST = 780
P_LNMIX, P_LNMLP, P_LNPE, P_GQ, P_GK, P_GOUT, P_RAW, P_GUP = 0, 8, 16, 24, 88, 152, 664, 672
NPRM = 928
NROT = 160
STAGES = None


class Res:
    __slots__ = ("name", "w", "r", "sem", "ndma", "psum")

    def __init__(self, name):
        self.name = name
        self.w = None
        self.r = {}
        self.sem = None
        self.ndma = 0
        self.psum = False


class Op:
    __slots__ = ("eng", "fn", "deps", "needs_inc", "tok", "dma_res")

    def __init__(self, eng, fn, deps, dma_res):
        self.eng = eng
        self.fn = fn
        self.deps = deps
        self.needs_inc = False
        self.tok = None
        self.dma_res = dma_res


class Prog:
    ENGS = ("pe", "act", "dve", "pool", "sp")

    def __init__(self, nc):
        self.nc = nc
        self.ops = []
        self.last = {}
        self.dmas = []
        self.capture = None

    def add(self, eng, fn, reads=(), writes=(), dma=None):
        if self.capture is not None:
            self.capture.append((eng, fn, list(reads), list(writes), dma))
            return None
        deps = []
        seen = set()

        def dep(o):
            if o is not None and id(o) not in seen:
                seen.add(id(o))
                deps.append(o)

        for R in reads:
            dep(R.w)
            if R.psum:
                for k_, o in R.r.items():
                    if k_ != eng:
                        dep(o)
        for R in writes:
            dep(R.w)
            for o in R.r.values():
                dep(o)
        op = Op(eng, fn, deps, dma)
        for R in reads:
            R.r[eng if dma is None else ("dma", id(op))] = op
        for R in writes:
            R.w = op
            R.r = {}
        self.ops.append(op)
        if dma is None:
            self.last[eng] = op
        else:
            self.dmas.append(op)
        return op

    def barrier(self):
        deps = list(self.last.values()) + self.dmas
        for e in self.ENGS:
            op = Op(e, None, list(deps), None)
            self.ops.append(op)
        self.dmas = []

    def emit(self):
        nc = self.nc
        engs = {"pe": nc.tensor, "act": nc.scalar, "dve": nc.vector, "pool": nc.gpsimd, "sp": nc.sync}
        esem = {e: nc.alloc_semaphore(name=f"sem_{e}") for e in ("pe", "act", "dve", "pool")}
        for op in self.ops:
            for d in op.deps:
                if d.dma_res is None:
                    if d.eng == "pe" and op.eng == "pe" and op.dma_res is None:
                        continue
                    d.needs_inc = True
        cnt = {e: 0 for e in esem}
        for op in self.ops:
            if op.dma_res is not None:
                R = op.dma_res
                if R.sem is None:
                    R.sem = nc.alloc_semaphore(name=f"dsem_{R.name}")
                R.ndma += 1
                op.tok = (R.sem, 16 * R.ndma)
            elif op.needs_inc:
                cnt[op.eng] += 1
                op.tok = (esem[op.eng], cnt[op.eng])
        per_eng = {e: [] for e in self.ENGS}
        last_dma = {}
        for op in self.ops:
            per_eng[op.eng].append(op)
            if op.dma_res is not None:
                last_dma[id(op.dma_res.sem)] = op.tok
        nw = 0
        for e in self.ENGS:
            eng = engs[e]
            waited = {}
            for op in per_eng[e]:
                need = {}
                for d in op.deps:
                    if d.tok is None:
                        continue
                    sem, val = d.tok
                    k = id(sem)
                    if waited.get(k, 0) >= val:
                        continue
                    if k not in need or need[k][1] < val:
                        need[k] = (sem, val)
                for k, (sem, val) in need.items():
                    eng.wait_ge(sem, val)
                    waited[k] = val
                    nw += 1
                if op.fn is None:
                    continue
                ins = op.fn(eng)
                if op.tok is not None:
                    ins.then_inc(op.tok[0], 16 if op.dma_res is not None else 1)
            if e == "sp":
                for k, (sem, val) in last_dma.items():
                    if waited.get(k, 0) < val:
                        eng.wait_ge(sem, val)
        return dict(n_ops=len(self.ops), n_waits=nw, cnt=cnt, n_dsem=len(last_dma))


class T:
    def __init__(self, ap, name):
        self.ap = ap
        self.res = Res(name)


def build(nc, NT, SEG=16, NL=2):
    NTOK = NT * 128
    NG = NT // 4
    P = Prog(nc)
    _n = [0]

    def sb(shape, dt, name=None):
        _n[0] += 1
        name = name or f"sb{_n[0]}"
        return T(nc.alloc_sbuf_tensor("s_" + name, list(shape), dt).ap(), name)

    def ps(shape, dt, name):
        t_ = T(nc.alloc_psum_tensor("p_" + name, list(shape), dt).ap(), name)
        t_.res.psum = True
        return t_

    def dram_in(name, shape, dt=F32):
        return nc.dram_tensor(name, list(shape), dt, kind="ExternalInput").ap()

    def dram_scr(name, shape, dt):
        return nc.dram_tensor(name, list(shape), dt, kind="Internal").ap()

    xin = dram_in("xin", [NTOK, D])
    ple = dram_in("ple", [NL, NTOK, 256])
    rot_d = dram_in("rot", [NTOK, NROT])
    link_d = dram_in("link", [128, 1])
    cst_d = dram_in("cst", [128, NCST])
    wmask_d = dram_in("wmask", [128, 17 * 128])
    prm_d = dram_in("prm", [NL, 128, NPRM])
    win_d = dram_in("w_in", [NL, 128, 8, NIN])
    wpc_d = dram_in("w_pc", [NL, NPIECE, 128, PW])
    yout = nc.dram_tensor("yout", [NTOK, D], F32, kind="ExternalOutput").ap()
    wbf = dram_scr("wbf", [NL, NPIECE, 128, PW], BF16)
    kT_s = dram_scr("kT_s", [NT, 128, 512], BF16)
    v_s = dram_scr("v_s", [NT, 128, 520], BF16)
    ob_s = dram_scr("ob_s", [NT, 128, 512], F32)
    mix_s = dram_scr("mix_s", [NT, 128, 1024], BF16)
    h1_s = dram_scr("h1_s", [NTOK, D], F32)

    cst = sb([128, NCST], F32, "cst")
    identb = sb([128, 128], BF16, "identb")
    Wm = sb([128, 17 * 128], BF16, "Wm")
    Wx = sb([128, 17 * 128], BF16, "Wx")
    prm = [sb([128, NPRM], F32, f"prm{l}") for l in range(NL)]
    linkc = sb([128, 1], F32, "link")
    arA = nc.alloc_sbuf_tensor("arA", [128, 8 * NIN], BF16).ap()
    arB = nc.alloc_sbuf_tensor("arB", [128, 19200], BF16).ap()
    arC = nc.alloc_sbuf_tensor("arC", [128, 8192], F32).ap()
    Win = T(arA.rearrange("p (k n) -> p k n", k=8), "Win")
    hid = T(arA[:, 0:16384].rearrange("p (f t) -> p f t", f=32), "hid")
    wring = [T(arA[:, 16384 + i * PW:16384 + (i + 1) * PW], f"wring{i}") for i in range(2)]
    KTr = [T(arB[:, i * 512:(i + 1) * 512].rearrange("p (c t) -> p c t", c=4), f"KT{i}") for i in range(RING)]
    Vr = [T(arB[:, 9216 + i * 520:9216 + (i + 1) * 520].rearrange("p (h e) -> p h e", h=8), f"V{i}") for i in range(RING)]
    hT = T(arB[:, 0:8192].bitcast(F32).rearrange("p (k t) -> p k t", k=8), "hT")
    xT = T(arB[:, 8192:12288].rearrange("p (k t) -> p k t", k=8), "xT")
    mixTg = T(arB[:, 12288:16384].rearrange("p (k t) -> p k t", k=8), "mixTg")
    mixTp = [T(mixTg.ap[:, :, i * 128:(i + 1) * 128], f"mixTp{i}") for i in range(4)]
    wst = [T(arC[:, i * 4096:(i + 1) * 4096], f"wst{i}") for i in range(2)]
    wring += [T(arC[:, i * 2048:(i + 1) * 2048].bitcast(BF16), f"wring{i + 2}") for i in range(2)]
    wcb = [T(arB[:, i * PW:(i + 1) * PW], f"wcb{i}") for i in range(2)]
    def cF(off, n):
        return arC[:, off:off + n]
    def cB(off, n):
        return arC[:, off:off + n // 2].bitcast(BF16)
    hld = [sb([128, D], F32, f"hld{i}") for i in range(2)]
    rotb = [sb([128, NROT], F32, f"rot{i}") for i in range(2)]
    obld = [T(cF(5120 + i * 512, 512), f"obld{i}") for i in range(2)]
    ub = sb([128, D], BF16, "ub")
    uT = [sb([128, 8, 128], BF16, f"uT{i}") for i in range(2)]
    ss1 = sb([128, 1], F32, "ss1")
    rs1 = sb([128, 1], F32, "rs1")
    junk = T(cF(0, 1024), "junk")
    zf = T(cF(1024, 512), "zf")
    zf_def = zf
    zsq = T(cF(1536, 512), "zsq")
    ss8 = sb([128, 8], F32, "ss8")
    rs8 = sb([128, 8], F32, "rs8")
    rA_def = T(cF(2048, 512), "rA")
    rB_def = T(cF(2560, 512), "rB")
    zfR = T(cF(3072, 512), "zfR")
    rAR = T(cF(3584, 512), "rAR")
    rBR = T(cF(4096, 512), "rBR")
    qkr = sb([128, 512], BF16, "qkr")
    QTd = [sb([128, 4, 128], BF16, f"QT{i}") for i in range(2)]
    KTst = [sb([128, 512], BF16, f"KTst{i}") for i in range(2)]
    Vst = [sb([128, 8, 65], BF16, f"Vst{i}") for i in range(2)]
    Pe = [[T(cB(3072 + (2 * i + j) * 256, 512).rearrange("p (c t) -> p c t", c=4), f"Pe{i}{j}") for j in range(2)] for i in range(2)]
    Pm = [[T(cB(4096 + (2 * i + j) * 256, 512).rearrange("p (c t) -> p c t", c=4), f"Pm{i}{j}") for j in range(2)] for i in range(2)]
    rden = sb([128, 8], F32, "rden")
    oa = sb([128, 8, 64], BF16, "oa")
    mixT = [sb([128, 8, 128], BF16, f"mixT{i}") for i in range(2)]
    bqk = sb([128, 256], F32, "bqk")
    bvb = sb([128, 256], BF16, "bvb")
    bg = sb([128, 32], F32, "bg")
    glrT = sb([33, 128], F32, "glrT")
    gex = sb([128, 128], F32, "gex")
    gsp = sb([128, 128], F32, "gsp")
    geq = sb([128, 128], F32, "geq")
    gek = sb([128, 128], F32, "gek")
    getot = sb([128, 1], F32, "getot")
    gqt = sb([128, 128], BF16, "gqt")
    gkt = sb([128, 128], BF16, "gkt")
    Qbd = sb([128, 4, 128], BF16, "Qbd")
    gqT = sb([128, 128], BF16, "gqT")
    gkT = sb([128, 128], BF16, "gkT")
    attm = sb([128, 4, 128], BF16, "attm")
    S32 = sb([128, 256], F32, "S32")
    Sbf = sb([128, 256], BF16, "Sbf")
    gtmp = sb([128, 256], F32, "gtmp")
    crot = sb([128, 512], BF16, "crot")
    cvb = sb([128, 256], BF16, "cvb")
    cvd = sb([128, 4, 64], BF16, "cvd")
    cqT = sb([64, 4, 128], BF16, "cqT")
    ckT = sb([64, 4, 128], BF16, "ckT")
    scD = sb([128, 4, 128], BF16, "scD")
    R32 = sb([64, 4, 64], F32, "R32")
    Rbf = sb([64, 4, 64], BF16, "Rbf")
    Dtab = [sb([128, 4, 128], F32, f"Dtab{d}") for d in range(2)]
    lgam = sb([128, 8], F32, "lgam")
    nlgam = sb([128, 8], F32, "nlgam")
    qdec = [sb([128, 4], F32, f"qdec{d}") for d in range(2)]
    kdec = [sb([128, 4], F32, f"kdec{d}") for d in range(2)]
    cdec = [sb([128, 4], F32, f"cdec{d}") for d in range(2)]
    rtmp = sb([128, 256], F32, "rtmp")
    obst = [sb([128, 512], F32, f"obst{i}") for i in range(2)]
    obc = sb([128, 512], F32, "obc")
    gx = T(cF(6144, 512), "gx")
    gs = T(cF(6656, 512), "gs")
    mixbc = sb([128, 512], BF16, "mixbc")
    gqs = sb([128, 64], F32, "gqs")
    pleld = [T(arA[:, 24576 + i * 512:24576 + (i + 1) * 512].bitcast(F32), f"pleld{i}") for i in range(2)]
    pleb = T(arA[:, 25600:25856], "pleb")
    pleT = T(arA[:, 25856:26880].rearrange("p (k t) -> p k t", k=2), "pleT")
    ost = [T(cF(4096 + i * 1024, 1024), f"ost{i}") for i in range(2)]
    sqb = [T(cF(6144 + i * 512, 512), f"sqb{i}") for i in range(2)]
    rstdT = T(cF(7168, 512), "rstdT")
    ones_b = sb([128, 128], BF16, "ones_b")
    sqh = [T(arB[:, 16384 + i * 512:16384 + (i + 1) * 512], f"sqh{i}") for i in range(2)]
    sql = [T(arB[:, 17408 + i * 512:17408 + (i + 1) * 512], f"sql{i}") for i in range(2)]

    B0 = ps([128, 512], F32, "B0")
    B1 = ps([128, 512], F32, "B1")
    B2 = ps([128, 512], F32, "B2")
    B3 = ps([128, 512], F32, "B3")
    B4 = ps([128, 512], F32, "B4")
    B5 = ps([128, 512], F32, "B5")
    B6 = ps([128, 1024], BF16, "B6")
    B7 = ps([128, 512], F32, "B7")

    def R(*ts):
        return [t.res for t in ts]

    def mm(out, lhsT, rhs, reads, writes, start=True, stop=True, skip=False):
        P.add("pe", lambda e: e.matmul(out, lhsT=lhsT, rhs=rhs, start=start, stop=stop, skip_group_check=skip),
              reads=R(*reads), writes=R(*writes))

    def trp(out, in_, ident, reads, writes):
        P.add("pe", lambda e: e.transpose(out, in_, ident), reads=R(*reads), writes=R(*writes))

    def act(out, in_, func, reads, writes, scale=None, bias=None):
        kw = {}
        if scale is not None:
            kw["scale"] = scale
        if bias is not None:
            kw["bias"] = bias
        P.add("act", lambda e: e.activation(out=out, in_=in_, func=func, **kw), reads=R(*reads), writes=R(*writes))

    def cp(eng, out, in_, reads, writes):
        if eng == "act":
            P.add("act", lambda e: e.copy(out=out, in_=in_), reads=R(*reads), writes=R(*writes))
        else:
            P.add(eng, lambda e: e.tensor_copy(out=out, in_=in_), reads=R(*reads), writes=R(*writes))

    def tt(eng, out, in0, in1, op, reads, writes):
        P.add(eng, lambda e: e.tensor_tensor(out=out, in0=in0, in1=in1, op=op), reads=R(*reads), writes=R(*writes))

    def ts(eng, out, in0, s1, op0, reads, writes):
        P.add(eng, lambda e: e.tensor_scalar(out=out, in0=in0, scalar1=s1, scalar2=None, op0=op0),
              reads=R(*reads), writes=R(*writes))

    def stt(out, in0, scalar, in1, op0, op1, reads, writes):
        P.add("dve", lambda e: e.scalar_tensor_tensor(out=out, in0=in0, scalar=scalar, in1=in1, op0=op0, op1=op1),
              reads=R(*reads), writes=R(*writes))

    def ascale(out, in_, scale_ap, reads, writes):
        P.add("act", lambda e: e.activation(out=out, in_=in_, func=AF.Identity, scale=scale_ap), reads=R(*reads), writes=R(*writes))

    def red(out, in_, reads, writes):
        P.add("dve", lambda e: e.tensor_reduce(out=out, in_=in_, axis=AX.X, op=ALU.add), reads=R(*reads), writes=R(*writes))

    def rcp(out, in_, reads, writes):
        P.add("dve", lambda e: e.reciprocal(out=out, in_=in_), reads=R(*reads), writes=R(*writes))

    def dma(out, in_, reads, writes, sem):
        P.add("sp", lambda e: e.dma_start(out=out, in_=in_), reads=R(*reads), writes=R(*writes), dma=sem.res)

    def mset(eng, ap, val, writes):
        P.add(eng, lambda e: e.memset(ap, val), writes=R(*writes))

    def rstd_from_ss(out_t, ss_t, n, width):
        act(out_t.ap[:, 0:width], ss_t.ap[:, 0:width], AF.Ln, [ss_t], [out_t], scale=1.0 / n, bias=EPS)
        act(out_t.ap[:, 0:width], out_t.ap[:, 0:width], AF.Exp, [out_t], [out_t], scale=-0.5)

    def sigmoid_chain(out_ap, in_ap, tmp_ap, reads, tmp_t, out_t):
        act(tmp_ap, in_ap, AF.Exp, reads, [tmp_t], scale=-1.0)
        act(tmp_ap, tmp_ap, AF.Ln, [tmp_t], [tmp_t], bias=1.0)
        act(out_ap, tmp_ap, AF.Exp, [tmp_t], [out_t], scale=-1.0)

    c = cst.ap
    identf = c[:, K_ID:K_ID + 128]

    dma(cst.ap, cst_d, [], [cst], cst)
    dma(linkc.ap, link_d, [], [linkc], linkc)
    for l in range(NL):
        dma(prm[l].ap, prm_d[l], [], [prm[l]], prm[l])
    cp("dve", identb.ap, identf, [cst], [identb])
    mset("pool", ones_b.ap, 1.0, [ones_b])
    for i in range(2):
        dma(wst[i].ap[:, 0:1088], wmask_d[:, i * 1088:(i + 1) * 1088], [], [wst[i]], wst[i])
        cp("dve", Wm.ap[:, i * 1088:(i + 1) * 1088], wst[i].ap[:, 0:1088], [wst[i]], [Wm])
        ts("pool", Wx.ap[:, i * 1088:(i + 1) * 1088], wst[i].ap[:, 0:1088], linkc.ap[:, 0:1], ALU.mult, [wst[i], linkc], [Wx])
    mset("pool", glrT.ap[32:33, :], 1.0, [glrT])
    for i in range(2):
        mset("pool", Vst[i].ap[:, :, 64:65], 1.0, [Vst[i]])

    cast_engs = ["dve", "act", "dve"]
    ce = 0
    for l in range(NL):
        for pc in range(NPIECE):
            s = (l * NPIECE + pc) % 2
            dma(wst[s].ap, wpc_d[l, pc], [], [wst[s]], wst[s])
            if 2 <= pc <= 9 or 18 <= pc <= 19:
                goff = P_LNMLP if pc <= 9 else P_LNPE
                for k in range(8):
                    if k % 2 == 0:
                        ts("dve", wcb[s].ap[:, k * 512:(k + 1) * 512], wst[s].ap[:, k * 512:(k + 1) * 512],
                           prm[l].ap[:, goff + k:goff + k + 1], ALU.mult, [wst[s], prm[l]], [wcb[s]])
                    else:
                        ascale(wcb[s].ap[:, k * 512:(k + 1) * 512], wst[s].ap[:, k * 512:(k + 1) * 512],
                               prm[l].ap[:, goff + k:goff + k + 1], [wst[s], prm[l]], [wcb[s]])
            else:
                for hh in range(2):
                    eng = cast_engs[ce % 3]
                    ce += 1
                    cp(eng, wcb[s].ap[:, hh * 2048:(hh + 1) * 2048], wst[s].ap[:, hh * 2048:(hh + 1) * 2048], [wst[s]], [wcb[s]])
            dma(wbf[l, pc], wcb[s].ap, [wcb[s]], [], wcb[s])
    P.barrier()

    def load_win(l):
        for k in range(8):
            s = k % 2
            dma(wst[s].ap[:, 0:NIN], win_d[l, :, k, :], [], [wst[s]], wst[s])
            half = NIN // 2
            ts("dve", Win.ap[:, k, 0:half], wst[s].ap[:, 0:half], prm[l].ap[:, P_LNMIX + k:P_LNMIX + k + 1], ALU.mult,
               [wst[s], prm[l]], [Win])
            ascale(Win.ap[:, k, half:NIN], wst[s].ap[:, half:NIN], prm[l].ap[:, P_LNMIX + k:P_LNMIX + k + 1],
                   [wst[s], prm[l]], [Win])

    def layer_tables(l):
        pr = prm[l]
        P.add("act", lambda e: e.mul(out=gqs.ap, in_=pr.ap[:, P_GQ:P_GQ + 64], mul=0.125), reads=R(pr), writes=R(gqs))
        act(lgam.ap, pr.ap[:, P_RAW:P_RAW + 8], AF.Exp, [pr], [lgam], scale=-1.0)
        act(lgam.ap, lgam.ap, AF.Ln, [lgam], [lgam], bias=1.0)
        P.add("act", lambda e: e.mul(out=nlgam.ap, in_=lgam.ap, mul=1.0), reads=R(lgam), writes=R(nlgam))
        P.add("act", lambda e: e.mul(out=lgam.ap, in_=lgam.ap, mul=-1.0), reads=R(lgam, nlgam), writes=R(lgam))
        for d in range(2):
            for h in range(4):
                lg = lgam.ap[:, d * 4 + h:d * 4 + h + 1]
                nlg = nlgam.ap[:, d * 4 + h:d * 4 + h + 1]
                act(Dtab[d].ap[:, h, :], c[:, K_TMS:K_TMS + 128], AF.Exp, [cst, lgam, nlgam], [Dtab[d]], scale=(lg if d == 0 else nlg))
                qsrc = c[:, K_TP1:K_TP1 + 1] if d == 0 else c[:, K_CT:K_CT + 1]
                ksrc = c[:, K_CM1:K_CM1 + 1] if d == 0 else c[:, K_S0:K_S0 + 1]
                act(qdec[d].ap[:, h:h + 1], qsrc, AF.Exp, [cst, lgam], [qdec[d]], scale=lg)
                act(kdec[d].ap[:, h:h + 1], ksrc, AF.Exp, [cst, lgam], [kdec[d]], scale=lg)
            for h in range(4):
                lg = lgam.ap[:, d * 4 + h:d * 4 + h + 1]
                P.add("act", lambda e, o=cdec[d].ap[:, h:h + 1], lg=lg: e.activation(out=o, in_=lg, func=AF.Exp, scale=128.0),
                      reads=R(lgam), writes=R(cdec[d]))
            mk = c[:, K_MF:K_MF + 128] if d == 0 else c[:, K_MB:K_MB + 128]
            tt("dve", Dtab[d].ap, Dtab[d].ap, mk.unsqueeze(1).to_broadcast([128, 4, 128]), ALU.mult, [Dtab[d], cst], [Dtab[d]])

    def front_load(t, slot, hsrc):
        h = hld[slot]
        dma(h.ap, hsrc[t * 128:(t + 1) * 128, :], [], [h], h)
        dma(rotb[slot].ap, rot_d[t * 128:(t + 1) * 128, :], [], [rotb[slot]], rotb[slot])

    def front(l, t, slot, hsrc):
        h = hld[slot]
        P.add("act", lambda e: e.activation(out=junk.ap, in_=h.ap, func=AF.Square, accum_out=ss1.ap),
              reads=R(h), writes=R(junk, ss1))
        rstd_from_ss(rs1, ss1, D, 1)
        ts("dve", ub.ap, h.ap, rs1.ap[:, 0:1], ALU.mult, [h, rs1], [ub])
        for k in range(8):
            trp(B6.ap[:, k * 128:(k + 1) * 128], ub.ap[:, k * 128:(k + 1) * 128], identb.ap, [ub, identb], [B6])
        cp("act", uT[slot].ap, B6.ap.rearrange("p (k t) -> p k t", k=8), [B6], [uT[slot]])

    def inproj(slot, c0, w, bank):
        for k in range(8):
            mm(bank.ap[:, 0:w], uT[slot].ap[:, k, :], Win.ap[:, k, c0:c0 + w], [uT[slot], Win], [bank], start=(k == 0), stop=(k == 7))

    def rotary(src, dst, half, cc, ns, rt, rA=None, rB=None):
        rA = rA or rA_def
        rB = rB or rB_def
        rd = 2 * half
        a3 = rA.ap.rearrange("p (h d) -> p h d", h=8)
        b3 = rB.ap.rearrange("p (h d) -> p h d", h=8)
        tt("dve", a3[:, :, 0:rd], src.ap.rearrange("p (h d) -> p h d", h=8)[:, :, 0:rd],
           cc.unsqueeze(1).to_broadcast([128, 8, rd]), ALU.mult, [src, rt], [rA])
        s3 = src.ap.rearrange("p (h d) -> p h d", h=8)
        tt("pool", b3[:, :, 0:half], s3[:, :, half:rd], ns[:, 0:half].unsqueeze(1).to_broadcast([128, 8, half]), ALU.mult, [src, rt], [rB])
        tt("dve", b3[:, :, half:rd], s3[:, :, 0:half], ns[:, half:rd].unsqueeze(1).to_broadcast([128, 8, half]), ALU.mult, [src, rt], [rB])
        d3 = dst.ap.rearrange("p (h d) -> p h d", h=8)
        tt("dve", d3[:, :, 0:rd], a3[:, :, 0:rd], b3[:, :, 0:rd], ALU.add, [rA, rB], [dst])
        if rd < 64:
            cp("act", d3[:, :, rd:64], s3[:, :, rd:64], [src], [dst])

    def qk_norm_rot(bank, gain_ap, gain_t, rt, dst):
        cp("act", zf.ap, bank.ap, [bank], [zf])
        act(zsq.ap, zf.ap, AF.Square, [zf], [zsq])
        red(ss8.ap, zsq.ap.rearrange("p (h d) -> p h d", h=8), [zsq], [ss8])
        rstd_from_ss(rs8, ss8, 64, 8)
        z3 = zf.ap.rearrange("p (h d) -> p h d", h=8)
        tt("dve", z3, z3, rs8.ap.unsqueeze(2).to_broadcast([128, 8, 64]), ALU.mult, [zf, rs8], [zf])
        tt("dve", z3, z3, gain_ap.unsqueeze(1).to_broadcast([128, 8, 64]), ALU.mult, [zf, gain_t], [zf])
        rotary(zf, dst, 8, rt.ap[:, 0:16], rt.ap[:, 16:32], rt)

    def c_rot(bank, rt, zf=None, rA=None, rB=None):
        zf = zf or zf_def
        cp("act", zf.ap[:, 0:256], bank.ap[:, 0:256], [bank], [zf])
        P.add("act", lambda e: e.mul(out=zf.ap[:, 256:512], in_=bank.ap[:, 256:512], mul=0.125), reads=R(bank), writes=R(zf))
        rotary(zf, crot, 32, rt.ap[:, 32:96], rt.ap[:, 96:160], rt, rA, rB)

    def state_link(d):
        ts("pool", S32.ap, S32.ap, linkc.ap[:, 0:1], ALU.mult, [S32, linkc], [S32])
        cp("act", Sbf.ap, S32.ap, [S32], [Sbf])
        ts("pool", R32.ap, R32.ap, linkc.ap[0:64, 0:1], ALU.mult, [R32, linkc], [R32])
        cp("act", Rbf.ap, R32.ap, [R32], [Rbf])

    _gl = [0]

    def GL():
        _gl[0] += 1
        lim = 10 ** 9 if STAGES is None else STAGES.get("glim", 10 ** 9)
        return _gl[0] <= lim

    def gla(l, d, tb_t=None, tb_ap=None):
        pr = prm[l]
        if tb_t is None:
            tb_t, tb_ap = B6, B6.ap[:, 512:768]
        _gl[0] = 0
        b1a = B1.ap[:, 0:256]
        if GL(): trp(B1.ap[0:32, 256:384], bg.ap, identf, [bg, cst], [B1])
        if GL(): cp("act", glrT.ap[0:32, :], B1.ap[0:32, 256:384], [B1], [glrT])
        if GL(): mm(B1.ap[:, 384:512], glrT.ap[0:33, :], pr.ap[0:33, P_GUP + d * 128:P_GUP + (d + 1) * 128], [glrT, pr], [B1])
        if GL(): act(gex.ap, B1.ap[:, 384:512], AF.Exp, [B1], [gex], scale=-1.0)
        if GL(): act(gsp.ap, gex.ap, AF.Ln, [gex], [gsp], bias=1.0)
        tri = c[:, K_TRII:K_TRII + 128] if d == 0 else c[:, K_TRIE:K_TRIE + 128]
        if GL(): mm(B1.ap[:, 256:384], tri, gsp.ap, [cst, gsp], [B1])
        if GL(): mm(B1.ap[:, 384:385], gsp.ap, c[:, K_NEG16:K_NEG16 + 1], [gsp, cst], [B1])
        sq_ = 1.0 if d == 0 else -1.0
        if GL(): act(geq.ap, B1.ap[:, 256:384], AF.Exp, [B1], [geq], scale=sq_)
        if GL(): act(gek.ap, B1.ap[:, 256:384], AF.Exp, [B1], [gek], scale=-sq_)
        if GL(): act(getot.ap, B1.ap[:, 384:385], AF.Exp, [B1], [getot])
        if GL(): stt(gqt.ap, bqk.ap[:, 0:128], 32.0 ** -0.5, geq.ap, ALU.mult, ALU.mult, [bqk, geq], [gqt])
        if GL(): tt("pool", gkt.ap, bqk.ap[:, 128:256], gek.ap, ALU.mult, [bqk, gek], [gkt])
        if GL(): trp(tb_ap[:, 0:128], gqt.ap, identb.ap, [gqt, identb], [tb_t])
        if GL(): trp(tb_ap[:, 128:256], gkt.ap, identb.ap, [gkt, identb], [tb_t])
        if GL(): tt("dve", Qbd.ap, tb_ap[:, 0:128].unsqueeze(1).to_broadcast([128, 4, 128]),
           c[:, K_BD:K_BD + 4].unsqueeze(2).to_broadcast([128, 4, 128]), ALU.mult, [tb_t, cst], [Qbd])
        if GL(): cp("act", gqT.ap, tb_ap[:, 0:128], [tb_t], [gqT])
        if GL(): cp("act", gkT.ap, tb_ap[:, 128:256], [tb_t], [gkT])
        if GL(): mm(B7.ap, gkT.ap, Qbd.ap.rearrange("p h t -> p (h t)"), [gkT, Qbd], [B7])
        mk = c[:, K_MF:K_MF + 128] if d == 0 else c[:, K_MB:K_MB + 128]
        if GL(): tt("dve", attm.ap, B7.ap.rearrange("p (h t) -> p h t", h=4), mk.unsqueeze(1).to_broadcast([128, 4, 128]), ALU.mult,
           [B7, cst], [attm])
        if d == 1:
            if GL(): ascale(S32.ap, S32.ap, getot.ap[:, 0:1], [S32, getot], [S32])
            if GL(): cp("act", Sbf.ap, S32.ap, [S32], [Sbf])
        if GL(): mm(b1a, gqT.ap, Sbf.ap, [gqT, Sbf], [B1], start=True, stop=False, skip=True)
        for h in range(4):
            if GL(): mm(B1.ap[:, 64 * h:64 * h + 64], attm.ap[:, h, :], bvb.ap[:, 64 * h:64 * h + 64], [attm, bvb], [B1],
               start=False, stop=(h == 3), skip=True)

    def gla_state(d, o_consumed_reads):
        mm(B1.ap[:, 256:512], gkt.ap, bvb.ap, [gkt, bvb], [B1])
        tt("dve", gtmp.ap.rearrange("p (h v) -> p h v", h=4), B1.ap[:, 256:512].rearrange("p (h v) -> p h v", h=4),
           c[:, K_BD:K_BD + 4].unsqueeze(2).to_broadcast([128, 4, 64]), ALU.mult, [B1, cst], [gtmp])
        if d == 0:
            ascale(S32.ap, S32.ap, getot.ap[:, 0:1], [S32, getot], [S32])
            stt(S32.ap, gtmp.ap, getot.ap[:, 0:1], S32.ap, ALU.mult, ALU.add, [gtmp, getot, S32], [S32])
            cp("act", Sbf.ap, S32.ap, [S32], [Sbf])
        else:
            tt("dve", S32.ap, S32.ap, gtmp.ap, ALU.add, [S32, gtmp], [S32])

    def ret(d, dst_ap, dst_t, Bo=None, Bs=None):
        Bo = Bo or B1
        Bs = Bs or B7
        for h in range(4):
            trp(B6.ap[0:64, h * 128:(h + 1) * 128], crot.ap[:, 64 * h:64 * h + 64], identb.ap, [crot, identb], [B6])
        for h in range(4):
            trp(B6.ap[0:64, 512 + h * 128:512 + (h + 1) * 128], crot.ap[:, 256 + 64 * h:256 + 64 * h + 64], identb.ap, [crot, identb], [B6])
        cp("act", cqT.ap, B6.ap[0:64, 0:512].rearrange("p (h t) -> p h t", h=4), [B6], [cqT])
        cp("dve", ckT.ap, B6.ap[0:64, 512:1024].rearrange("p (h t) -> p h t", h=4), [B6], [ckT])
        for h in range(4):
            mm(Bs.ap[:, h * 128:(h + 1) * 128], ckT.ap[:, h, :], cqT.ap[:, h, :], [ckT, cqT], [Bs])
        tt("dve", scD.ap, Bs.ap.rearrange("p (h t) -> p h t", h=4), Dtab[d].ap, ALU.mult, [Bs, Dtab[d]], [scD])
        tt("pool", cvd.ap, cvb.ap.rearrange("p (h e) -> p h e", h=4), kdec[d].ap.unsqueeze(2).to_broadcast([128, 4, 64]), ALU.mult,
           [cvb, kdec[d]], [cvd])
        for h in range(4):
            mm(Bo.ap[:, 64 * h:64 * h + 64], scD.ap[:, h, :], cvb.ap[:, 64 * h:64 * h + 64], [scD, cvb], [Bo])
        for h in range(4):
            mm(Bo.ap[:, 256 + 64 * h:256 + 64 * h + 64], cqT.ap[:, h, :], Rbf.ap[:, h, :], [cqT, Rbf], [Bo])
        tt("dve", rtmp.ap.rearrange("p (h e) -> p h e", h=4), Bo.ap[:, 256:512].rearrange("p (h e) -> p h e", h=4),
           qdec[d].ap.unsqueeze(2).to_broadcast([128, 4, 64]), ALU.mult, [Bo, qdec[d]], [rtmp])
        tt("dve", dst_ap, rtmp.ap, Bo.ap[:, 0:256], ALU.add, [rtmp, Bo], [dst_t])
        for h in range(4):
            mm(Bo.ap[0:64, 64 * h:64 * h + 64], crot.ap[:, 256 + 64 * h:256 + 64 * h + 64], cvd.ap[:, h, :], [crot, cvd], [Bo])
        tt("dve", R32.ap, R32.ap, cdec[d].ap[0:64, :].unsqueeze(2).to_broadcast([64, 4, 64]), ALU.mult, [R32, cdec[d]], [R32])
        tt("dve", R32.ap, R32.ap, Bo.ap[0:64, 0:256].rearrange("p (h e) -> p h e", h=4), ALU.add, [R32, Bo], [R32])
        cp("act", Rbf.ap, R32.ap, [R32], [Rbf])

    def reset_states():
        mset("pool", S32.ap, 0.0, [S32])
        mset("pool", Sbf.ap, 0.0, [Sbf])
        mset("pool", R32.ap, 0.0, [R32])
        mset("pool", Rbf.ap, 0.0, [Rbf])

    def pass1(l, hsrc):
        reset_states()
        order = list(range(NT - 1, -1, -1))
        front_load(order[0], 0, hsrc)
        for i, t in enumerate(order):
            slot = i % 2
            front(l, t, slot, hsrc)
            if i + 1 < NT:
                front_load(order[i + 1], (i + 1) % 2, hsrc)
            rt = rotb[slot]
            if t % SEG == SEG - 1 and t != NT - 1:
                state_link(1)
            ob = obst[slot]
            P.capture = []
            inproj(slot, C_AK, 512, B5)
            qk_norm_rot(B5, prm[l].ap[:, P_GK:P_GK + 64], prm[l], rt, qkr)
            b5b = B5.ap.bitcast(BF16)
            for cc_ in range(4):
                trp(b5b[:, cc_ * 128:(cc_ + 1) * 128], qkr.ap[:, cc_ * 128:(cc_ + 1) * 128], identb.ap, [qkr, identb], [B5])
            cp("dve", KTst[slot].ap, b5b[:, 0:512], [B5], [KTst[slot]])
            dma(kT_s[t], KTst[slot].ap, [KTst[slot]], [], KTst[slot])
            inproj(slot, C_AV, 512, B5)
            cp("act", Vst[slot].ap[:, :, 0:64], B5.ap.rearrange("p (h e) -> p h e", h=8), [B5], [Vst[slot]])
            dma(v_s[t], Vst[slot].ap.rearrange("p h e -> p (h e)"), [Vst[slot]], [], Vst[slot])
            chainKV = P.capture
            P.capture = []
            inproj(slot, C_B, 512, B0)
            cp("dve", bqk.ap, B0.ap[:, 0:256], [B0], [bqk])
            cp("act", bvb.ap, B0.ap[:, 256:512], [B0], [bvb])
            inproj(slot, C_G, 32, B0)
            cp("dve", bg.ap, B0.ap[:, 0:32], [B0], [bg])
            gla(l, 1, B0, B0.ap.bitcast(BF16)[:, 0:256])
            cp("act", ob.ap[:, 0:256], B1.ap[:, 0:256], [B1], [ob])
            gla_state(1, None)
            chainG = P.capture
            P.capture = []
            inproj(slot, C_CQK, 512, B2)
            c_rot(B2, rt, zfR, rAR, rBR)
            inproj(slot, C_CVG, 256, B2)
            cp("act", cvb.ap, B2.ap[:, 0:256], [B2], [cvb])
            ret(1, ob.ap[:, 256:512], ob, B3, B4)
            chainR = P.capture
            P.capture = None
            chains = [chainG, chainR, chainKV]
            pos_ = [0, 0, 0]
            while any(pos_[i_] < len(chains[i_]) for i_ in range(3)):
                for i_ in range(3):
                    if pos_[i_] < len(chains[i_]):
                        P.add(*chains[i_][pos_[i_]])
                        pos_[i_] += 1
            dma(ob_s[t], ob.ap, [ob], [], ob)

    def load_kv(kt):
        s = kt % RING
        dma(KTr[s].ap.rearrange("p c t -> p (c t)"), kT_s[kt], [], [KTr[s]], KTr[s])
        dma(Vr[s].ap.rearrange("p h e -> p (h e)"), v_s[kt], [], [Vr[s]], Vr[s])

    def pass2a(l, hsrc):
        reset_states()
        for kt in range(0, min(9, NT)):
            load_kv(kt)
        def q_front(t_, slot_):
            front(l, t_, slot_, hsrc)
            inproj(slot_, C_AQ, 512, B0)
            qk_norm_rot(B0, gqs.ap, gqs, rotb[slot_], qkr)
            for cc_ in range(4):
                trp(B6.ap[:, cc_ * 128:(cc_ + 1) * 128], qkr.ap[:, cc_ * 128:(cc_ + 1) * 128], identb.ap, [qkr, identb], [B6])
            cp("dve", QTd[slot_].ap, B6.ap[:, 0:512].rearrange("p (c t) -> p c t", c=4), [B6], [QTd[slot_]])

        front_load(0, 0, hsrc)
        if NT > 1:
            front_load(1, 1, hsrc)
        dma(obld[0].ap, ob_s[0], [], [obld[0]], obld[0])
        q_front(0, 0)
        for t in range(NT):
            slot = t % 2
            QT = QTd[slot]
            if t + 9 < NT:
                load_kv(t + 9)
            if t + 1 < NT:
                dma(obld[(t + 1) % 2].ap, ob_s[t + 1], [], [obld[(t + 1) % 2]], obld[(t + 1) % 2])
            rt = rotb[slot]
            if t % SEG == 0 and t != 0:
                state_link(0)
            mt = mixT[slot]
            P.capture = []
            inproj(slot, C_B, 512, B0)
            cp("dve", bqk.ap, B0.ap[:, 0:256], [B0], [bqk])
            cp("act", bvb.ap, B0.ap[:, 256:512], [B0], [bvb])
            inproj(slot, C_G, 288, B0)
            cp("dve", bg.ap, B0.ap[:, 0:32], [B0], [bg])
            cp("act", gx.ap[:, 0:256], B0.ap[:, 32:288], [B0], [gx])
            gla(l, 0)
            tt("dve", obc.ap[:, 0:256], B1.ap[:, 0:256], obld[slot].ap[:, 0:256], ALU.add, [B1, obld[slot]], [obc])
            gla_state(0, None)
            inproj(slot, C_CQK, 512, B0)
            c_rot(B0, rt)
            inproj(slot, C_CVG, 512, B0)
            cp("act", cvb.ap, B0.ap[:, 0:256], [B0], [cvb])
            cp("dve", gx.ap[:, 256:512], B0.ap[:, 256:512], [B0], [gx])
            ret(0, rtmp.ap, rtmp)
            tt("dve", obc.ap[:, 256:512], rtmp.ap, obld[slot].ap[:, 256:512], ALU.add, [rtmp, obld[slot]], [obc])
            act(zsq.ap, obc.ap, AF.Square, [obc], [zsq])
            red(ss8.ap, zsq.ap.rearrange("p (h d) -> p h d", h=8), [zsq], [ss8])
            rstd_from_ss(rs8, ss8, 64, 8)
            o3 = obc.ap.rearrange("p (h d) -> p h d", h=8)
            tt("dve", o3, o3, rs8.ap.unsqueeze(2).to_broadcast([128, 8, 64]), ALU.mult, [obc, rs8], [obc])
            tt("dve", obc.ap, obc.ap, prm[l].ap[:, P_GOUT:P_GOUT + 512], ALU.mult, [obc, prm[l]], [obc])
            sigmoid_chain(gs.ap, gx.ap, gs.ap, [gx], gs, gs)
            tt("pool", gs.ap, gs.ap, gx.ap, ALU.mult, [gs, gx], [gs])
            tt("dve", mixbc.ap, obc.ap, gs.ap, ALU.mult, [obc, gs], [mixbc])
            for cc_ in range(4):
                trp(B6.ap[:, 512 + cc_ * 128:512 + (cc_ + 1) * 128], mixbc.ap[:, cc_ * 128:(cc_ + 1) * 128], identb.ap, [mixbc, identb], [B6])
            cp("act", mt.ap[:, 4:8, :], B6.ap[:, 512:1024].rearrange("p (c t) -> p c t", c=4), [B6], [mt])
            if t + 1 < NT:
                q_front(t + 1, (t + 1) % 2)
            side = P.capture
            P.capture = None
            side_pos = [0]

            def drip(n):
                while n > 0 and side_pos[0] < len(side):
                    P.add(*side[side_pos[0]])
                    side_pos[0] += 1
                    n -= 1

            def drain_side():
                drip(len(side))
            kts = list(range(max(0, t - 8), min(NT - 1, t + 8) + 1))
            first = [True, True]
            quota = -(-len(side) // len(kts))

            def pv(i, kt):
                s = kt % RING
                par = i % 2
                for cc_ in range(4):
                    for eo, bank in ((0, B4), (1, B5)):
                        hh = 2 * cc_ + eo
                        mm(bank.ap[:, cc_ * 65:(cc_ + 1) * 65], Pm[par][eo].ap[:, cc_, :], Vr[s].ap[:, hh, :],
                           [Pm[par][eo], Vr[s]], [bank], start=first[eo], stop=False, skip=True)
                        first[eo] = False

            for i, kt in enumerate(kts):
                s = kt % RING
                par = i % 2
                dlt = kt - t
                cross = (kt // SEG) != (t // SEG)
                mtab = Wx if cross else Wm
                mk = mtab.ap[:, (dlt + 8) * 128:(dlt + 9) * 128].unsqueeze(1).to_broadcast([128, 4, 128])
                for cc_ in range(4):
                    mm(B2.ap[:, cc_ * 128:(cc_ + 1) * 128], KTr[s].ap[0:64, cc_, :], QT.ap[0:64, cc_, :], [KTr[s], QT], [B2])
                    mm(B3.ap[:, cc_ * 128:(cc_ + 1) * 128], KTr[s].ap[64:128, cc_, :], QT.ap[64:128, cc_, :], [KTr[s], QT], [B3])
                act(Pe[par][0].ap, B2.ap.rearrange("p (c t) -> p c t", c=4), AF.Exp, [B2], [Pe[par][0]])
                act(Pe[par][1].ap, B3.ap.rearrange("p (c t) -> p c t", c=4), AF.Exp, [B3], [Pe[par][1]])
                tt("dve", Pm[par][0].ap, Pe[par][0].ap, mk, ALU.mult, [Pe[par][0], mtab], [Pm[par][0]])
                tt("dve", Pm[par][1].ap, Pe[par][1].ap, mk, ALU.mult, [Pe[par][1], mtab], [Pm[par][1]])
                if i > 0:
                    pv(i - 1, kts[i - 1])
                drip(quota)
            pv(len(kts) - 1, kts[-1])
            oa4 = oa.ap.rearrange("p (c e) d -> p c e d", e=2)
            for eo, bank in ((0, B4), (1, B5)):
                b3 = bank.ap[:, 0:260].rearrange("p (c e) -> p c e", c=4)
                rcp(rden.ap[:, eo * 4:(eo + 1) * 4], b3[:, :, 64], [bank], [rden])
                tt("dve", oa4[:, :, eo, :], b3[:, :, 0:64], rden.ap[:, eo * 4:(eo + 1) * 4].unsqueeze(2).to_broadcast([128, 4, 64]),
                   ALU.mult, [bank, rden], [oa])
            oaf = oa.ap.rearrange("p h d -> p (h d)")
            for cc_ in range(4):
                trp(B6.ap[:, cc_ * 128:(cc_ + 1) * 128], oaf[:, cc_ * 128:(cc_ + 1) * 128], identb.ap, [oa, identb], [B6])
            mt = mixT[slot]
            cp("act", mt.ap[:, 0:4, :], B6.ap[:, 0:512].rearrange("p (c t) -> p c t", c=4), [B6], [mt])
            drain_side()
            dma(mix_s[t], mt.ap.rearrange("p k t -> p (k t)"), [mt], [], mt)
            if t + 2 < NT:
                front_load(t + 2, slot, hsrc)

    def pass2b(l, hsrc, hdst):
        wctr = [0]

        def wload(pc):
            s = wctr[0] % 4
            wctr[0] += 1
            dma(wring[s].ap, wbf[l, pc], [], [wring[s]], wring[s])
            return wring[s]

        def norm_to_xT():
            for k in range(8):
                sq, hi, lo = sqb[k % 2], sqh[k % 2], sql[k % 2]
                act(sq.ap, hT.ap[:, k, :], AF.Square, [hT], [sq])
                cp("act", hi.ap, sq.ap, [sq], [hi])
                tt("dve", lo.ap, sq.ap, hi.ap, ALU.subtract, [sq, hi], [lo])
                mm(B4.ap, ones_b.ap, hi.ap, [ones_b, hi], [B4], start=(k == 0), stop=False)
                mm(B4.ap, ones_b.ap, lo.ap, [ones_b, lo], [B4], start=False, stop=(k == 7))
            act(rstdT.ap, B4.ap, AF.Ln, [B4], [rstdT], scale=1.0 / D, bias=EPS)
            act(rstdT.ap, rstdT.ap, AF.Exp, [rstdT], [rstdT], scale=-0.5)
            for k in range(8):
                tt(["dve", "dve", "dve", "pool"][k % 4], xT.ap[:, k, :], hT.ap[:, k, :], rstdT.ap, ALU.mult, [hT, rstdT], [xT])

        for g in range(NG):
            for i in range(4):
                t = 4 * g + i
                slot = t % 2
                h = hld[slot]
                dma(h.ap, hsrc[t * 128:(t + 1) * 128, :], [], [h], h)
                dma(mixTp[i].ap, mix_s[t].rearrange("p (k t) -> p k t", k=8), [], [mixTp[i]], mixTp[i])
                pl = pleld[slot]
                dma(pl.ap, ple[l, t * 128:(t + 1) * 128, :], [], [pl], pl)
                for half in range(2):
                    bank = [B0, B1][half]
                    for kk in range(4):
                        k = half * 4 + kk
                        trp(bank.ap[:, kk * 128:(kk + 1) * 128], h.ap[:, k * 128:(k + 1) * 128], identf, [h, cst], [bank])
                    cp(["act", "dve"][half], hT.ap[:, half * 4:(half + 1) * 4, i * 128:(i + 1) * 128],
                       bank.ap.rearrange("p (k t) -> p k t", k=4), [bank], [hT])
                cp("dve", pleb.ap, pl.ap, [pl], [pleb])
                for k2 in range(2):
                    trp(B6.ap[:, k2 * 128:(k2 + 1) * 128], pleb.ap[:, k2 * 128:(k2 + 1) * 128], identb.ap, [pleb, identb], [B6])
                cp("act", pleT.ap[:, :, i * 128:(i + 1) * 128], B6.ap[:, 0:256].rearrange("p (k t) -> p k t", k=2), [B6], [pleT])
            for half in range(2):
                w = wload(0 + half)
                w3 = w.ap.rearrange("p (k n) -> p k n", k=8)
                for jj in range(4):
                    j = half * 4 + jj
                    bank = [B2, B3][j % 2]
                    for k in range(8):
                        mm(bank.ap, w3[:, k, jj * 128:(jj + 1) * 128], mixTg.ap[:, k, :], [w] + mixTp, [bank], start=(k == 0), stop=(k == 7))
                    tt(["dve", "pool"][0], hT.ap[:, j, :], hT.ap[:, j, :], bank.ap, ALU.add, [hT, bank], [hT])
            norm_to_xT()
            for fp in range(8):
                w = wload(2 + fp)
                w3 = w.ap.rearrange("p (k n) -> p k n", k=8)
                for ff in range(4):
                    f = fp * 4 + ff
                    bank = [B2, B3][f % 2]
                    for k in range(8):
                        mm(bank.ap, w3[:, k, ff * 128:(ff + 1) * 128], xT.ap[:, k, :], [w, xT], [bank], start=(k == 0), stop=(k == 7))
                    sq = sqb[f % 2]
                    act(sq.ap, bank.ap, AF.Relu, [bank], [sq])
                    tt("dve", hid.ap[:, f, :], sq.ap, bank.ap, ALU.mult, [sq, bank], [hid])
            for j in range(8):
                w = wload(10 + j)
                w3 = w.ap.rearrange("p (f n) -> p f n", f=32)
                bank = [B2, B3][j % 2]
                for f in range(32):
                    mm(bank.ap, w3[:, f, :], hid.ap[:, f, :], [w, hid], [bank], start=(f == 0), stop=(f == 31))
                tt("dve", hT.ap[:, j, :], hT.ap[:, j, :], bank.ap, ALU.add, [hT, bank], [hT])
            norm_to_xT()
            wp = wload(20)
            wp3 = wp.ap[:, 0:2048].rearrange("p (k n) -> p k n", k=2)
            for half in range(2):
                w = wload(18 + half)
                w3 = w.ap.rearrange("p (k n) -> p k n", k=8)
                for jj in range(4):
                    j = half * 4 + jj
                    bank = [B2, B3][j % 2]
                    for k in range(8):
                        mm(bank.ap, w3[:, k, jj * 128:(jj + 1) * 128], xT.ap[:, k, :], [w, xT], [bank], start=(k == 0), stop=(k == 7))
                    for k2 in range(2):
                        mm(B5.ap, wp3[:, k2, j * 128:(j + 1) * 128], pleT.ap[:, k2, :], [wp, pleT], [B5], start=(k2 == 0), stop=(k2 == 1))
                    sq = sqb[j % 2]
                    sigmoid_chain(sq.ap, bank.ap, sq.ap, [bank], sq, sq)
                    tt("dve", sq.ap, sq.ap, B5.ap, ALU.mult, [sq, B5], [sq])
                    tt("pool", hT.ap[:, j, :], hT.ap[:, j, :], sq.ap, ALU.add, [hT, sq], [hT])
            for i in range(4):
                t = 4 * g + i
                o = ost[t % 2]
                for half in range(2):
                    bank = [B0, B1][half]
                    for kk in range(4):
                        k = half * 4 + kk
                        trp(bank.ap[:, kk * 128:(kk + 1) * 128], hT.ap[:, k, i * 128:(i + 1) * 128], identf, [hT, cst], [bank])
                    cp(["act", "dve"][half], o.ap[:, half * 512:(half + 1) * 512], bank.ap, [bank], [o])
                dma(hdst[t * 128:(t + 1) * 128, :], o.ap, [o], [], o)

    for l in range(NL):
        hsrc = xin if l == 0 else h1_s
        hdst = h1_s if l == 0 else yout
        if STAGES is not None and l >= STAGES.get("nl", 2):
            break
        load_win(l)
        layer_tables(l)
        P.barrier()
        if STAGES is None or "p1" in STAGES:
            pass1(l, hsrc)
        P.barrier()
        if STAGES is None or "p2a" in STAGES:
            pass2a(l, hsrc)
        P.barrier()
        if STAGES is None or "p2b" in STAGES:
            pass2b(l, hsrc, hdst)
        P.barrier()
    if STAGES is not None:
        dma(hld[0].ap, xin[0:128, :], [], [hld[0]], hld[0])
        dma(yout[0:128, :], hld[0].ap, [hld[0]], [], hld[0])
    return P.emit()


def _consts():
    s = np.arange(128)[:, None]
    t = np.arange(128)[None, :]
    c = np.zeros((128, NCST), np.float32)
    c[:, K_ID:K_ID + 128] = np.eye(128)
    c[:, K_TRII:K_TRII + 128] = np.where(s <= t, -1.0 / 16, 0.0)
    c[:, K_TRIE:K_TRIE + 128] = np.where(s < t, -1.0 / 16, 0.0)
    c[:, K_MF:K_MF + 128] = (s <= t)
    c[:, K_MB:K_MB + 128] = (s > t)
    c[:, K_TMS:K_TMS + 128] = (t - s)
    p = np.arange(128)
    c[:, K_TP1] = p + 1
    c[:, K_CM1] = 127 - p
    c[:, K_CT] = 128 - p
    c[:, K_S0] = p
    c[:, K_NEG16] = -1.0 / 16
    c[:, K_ONE] = 1.0
    for h in range(4):
        c[:, K_BD + h] = (p // 32 == h)
    wm = np.zeros((128, 17 * 128), np.float32)
    for dl in range(-8, 9):
        off = dl * 128 + s - t
        a = np.abs(off)
        w = (a <= 64).astype(np.float32) + ((off % 4 == 0) & (a <= 256)) + ((off % 16 == 0) & (a <= 1024))
        wm[:, (dl + 8) * 128:(dl + 9) * 128] = w
    return c, wm


def _rot_table(pos):
    out = np.zeros((pos.shape[0], NROT), np.float32)

    def tab(rot_dim, theta):
        half = rot_dim // 2
        inv = (np.float32(1.0) / (np.float32(theta) ** (np.arange(half, dtype=np.float32) * np.float32(2.0 / rot_dim)))).astype(np.float32)
        ang = (pos[:, None].astype(np.float32) * inv[None, :]).astype(np.float32)
        cs = np.cos(ang.astype(np.float64)).astype(np.float32)
        sn = np.sin(ang.astype(np.float64)).astype(np.float32)
        return np.concatenate([cs, cs], 1), np.concatenate([-sn, sn], 1)

    cc, ns = tab(16, 500000.0)
    out[:, 0:16], out[:, 16:32] = cc, ns
    cc, ns = tab(64, 10000.0)
    out[:, 32:96], out[:, 96:160] = cc, ns
    return out


def _weights(inp, NL=2):
    f = np.float32
    w_in = np.asarray(inp["w_in"], f)
    perm = np.concatenate([np.arange(0, 2048), np.arange(2304, 2336), np.arange(2048, 2304), np.arange(2336, 3360)])
    win = np.ascontiguousarray(w_in[:, :, perm].reshape(NL, 8, 128, NIN).transpose(0, 2, 1, 3))
    pcs = np.zeros((NL, NPIECE, 128, PW), f)
    for l in range(NL):
        wo = np.asarray(inp["w_out"][l], f).reshape(8, 128, 1024).transpose(1, 0, 2)
        for hh in range(2):
            pcs[l, hh] = wo[:, :, hh * 512:(hh + 1) * 512].reshape(128, PW)
        w1 = np.asarray(inp["w_mlp_in"][l], f).reshape(8, 128, 4096).transpose(1, 0, 2)
        for fp in range(8):
            pcs[l, 2 + fp] = w1[:, :, fp * 512:(fp + 1) * 512].reshape(128, PW)
        w2 = np.asarray(inp["w_mlp_out"][l], f).reshape(32, 128, 1024).transpose(1, 0, 2)
        for j in range(8):
            pcs[l, 10 + j] = w2[:, :, j * 128:(j + 1) * 128].reshape(128, PW)
        wg = np.asarray(inp["w_pe_gate"][l], f).reshape(8, 128, 1024).transpose(1, 0, 2)
        for hh in range(2):
            pcs[l, 18 + hh] = wg[:, :, hh * 512:(hh + 1) * 512].reshape(128, PW)
        wp = np.asarray(inp["w_pe_proj"][l], f).reshape(2, 128, 1024).transpose(1, 0, 2)
        pcs[l, 20, :, 0:2048] = wp.reshape(128, 2048)
    prm = np.zeros((NL, 128, NPRM), f)
    for l in range(NL):
        prm[l, :, P_LNMIX:P_LNMIX + 8] = np.asarray(inp["ln_mix"][l], f).reshape(8, 128).T
        prm[l, :, P_LNMLP:P_LNMLP + 8] = np.asarray(inp["ln_mlp"][l], f).reshape(8, 128).T
        prm[l, :, P_LNPE:P_LNPE + 8] = np.asarray(inp["ln_pe"][l], f).reshape(8, 128).T
        prm[l, :, P_GQ:P_GQ + 64] = np.asarray(inp["attn_q_norm"][l], f)[None]
        prm[l, :, P_GK:P_GK + 64] = np.asarray(inp["attn_k_norm"][l], f)[None]
        prm[l, :, P_GOUT:P_GOUT + 256] = np.asarray(inp["gla_out_norm"][l], f)[None]
        prm[l, :, P_GOUT + 256:P_GOUT + 512] = np.asarray(inp["ret_out_norm"][l], f)[None]
        prm[l, :, P_RAW:P_RAW + 8] = np.asarray(inp["ret_decay_raw"][l], f).reshape(8)[None]
        gu = np.asarray(inp["gla_gate_up"][l], f)
        gb = np.asarray(inp["gla_gate_bias"][l], f)
        prm[l, 0:16, P_GUP:P_GUP + 128] = gu[0]
        prm[l, 16:32, P_GUP + 128:P_GUP + 256] = gu[1]
        prm[l, 32, P_GUP:P_GUP + 128] = gb[0]
        prm[l, 32, P_GUP + 128:P_GUP + 256] = gb[1]
    return win, pcs, prm


_CACHE = {}


def _program(NT):
    if NT not in _CACHE:
        nc = bass.Bass("TRN2", target_bir_lowering=False)
        stats = build(nc, NT)
        _CACHE[NT] = (nc, stats)
    return _CACHE[NT][0]


def run_cores(core_specs, inp, NT):
    cst, wm = _consts()
    win, pcs, prm = _weights(inp)
    in_maps = []
    for cs in core_specs:
        in_maps.append({
            "xin": np.ascontiguousarray(cs["x"], np.float32),
            "ple": np.ascontiguousarray(cs["ple"], np.float32),
            "rot": _rot_table(cs["pos"].astype(np.float32)),
            "link": np.full((128, 1), cs["link"], np.float32),
            "cst": cst, "wmask": wm, "prm": prm, "w_in": win, "w_pc": pcs,
        })
    nc = _program(NT)
    res = run_bass_kernel_spmd(nc, in_maps, core_ids=list(range(len(in_maps))))
    return [r["yout"] for r in res.results]


def kernel(**inp):
    xp = np.asarray(inp["x_prompt"], np.float32)
    xs = np.asarray(inp["x_sample"], np.float32)
    pp = np.asarray(inp["p_prompt"], np.float32)
    psm = np.asarray(inp["p_sample"], np.float32)
    NT = 128
    NTOK = NT * 128
    SL = 2048
    groups = [list(range(0, 6)), list(range(6, 12)), list(range(12, 17)), list(range(17, 22)), list(range(22, 27)), list(range(27, 32))]
    specs = []
    for b in range(2):
        specs.append(dict(x=xp[b], ple=pp[:, b], pos=np.arange(NTOK), link=1.0))
    for gsq in groups:
        x = np.zeros((NTOK, D), np.float32)
        pl = np.zeros((2, NTOK, 256), np.float32)
        for i, sq in enumerate(gsq):
            x[i * SL:(i + 1) * SL] = xs[sq]
            pl[:, i * SL:(i + 1) * SL] = psm[:, sq]
        specs.append(dict(x=x, ple=pl, pos=np.tile(np.arange(SL), NTOK // SL), link=0.0))
    outs = run_cores(specs, inp, NT)
    y_prompt = np.stack([outs[0], outs[1]], 0).astype(np.float32)
    y_sample = np.zeros_like(xs)
    for ci, gsq in enumerate(groups):
        for i, sq in enumerate(gsq):
            y_sample[sq] = outs[2 + ci][i * SL:(i + 1) * SL]
    return (y_prompt, y_sample)
```

```python
import numpy as np
import concourse.bass as bass
import concourse.mybir as mybir
from concourse.bass_utils import run_bass_kernel_spmd

F32 = mybir.dt.float32
BF16 = mybir.dt.bfloat16
AF = mybir.ActivationFunctionType
ALU = mybir.AluOpType
AX = mybir.AxisListType

D = 1024
NIN = 3360
EPS = 1e-6
NPIECE = 21
PW = 4096
RING = 18
C_AQ, C_AK, C_AV, C_B, C_G, C_CQK, C_CVG = 0, 512, 1024, 1536, 2048, 2336, 2848
K_ID, K_TRII, K_TRIE, K_MF, K_MB, K_TMS = 0, 128, 256, 384, 512, 640
K_TP1, K_CM1, K_CT, K_S0, K_NEG16, K_ONE, K_BD = 768, 769, 770, 771, 773, 774, 776
NCST = 780
P_LNMIX, P_LNMLP, P_LNPE, P_GQ, P_GK, P_GOUT, P_RAW, P_GUP = 0, 8, 16, 24, 88, 152, 664, 672
NPRM = 928
NROT = 160
STAGES = None


class Res:
    __slots__ = ("name", "w", "r", "sem", "ndma", "psum")

    def __init__(self, name):
        self.name = name
        self.w = None
        self.r = {}
        self.sem = None
        self.ndma = 0
        self.psum = False


class Op:
    __slots__ = ("eng", "fn", "deps", "needs_inc", "tok", "dma_res")

    def __init__(self, eng, fn, deps, dma_res):
        self.eng = eng
        self.fn = fn
        self.deps = deps
        self.needs_inc = False
        self.tok = None
        self.dma_res = dma_res


class Prog:
    ENGS = ("pe", "act", "dve", "pool", "sp")

    def __init__(self, nc):
        self.nc = nc
        self.ops = []
        self.last = {}
        self.dmas = []
        self.capture = None

    def add(self, eng, fn, reads=(), writes=(), dma=None):
        if self.capture is not None:
            self.capture.append((eng, fn, list(reads), list(writes), dma))
            return None
        deps = []
        seen = set()

        def dep(o):
            if o is not None and id(o) not in seen:
                seen.add(id(o))
                deps.append(o)

        for R in reads:
            dep(R.w)
            if R.psum:
                for k_, o in R.r.items():
                    if k_ != eng:
                        dep(o)
        for R in writes:
            dep(R.w)
            for o in R.r.values():
                dep(o)
        op = Op(eng, fn, deps, dma)
        for R in reads:
            R.r[eng if dma is None else ("dma", id(op))] = op
        for R in writes:
            R.w = op
            R.r = {}
        self.ops.append(op)
        if dma is None:
            self.last[eng] = op
        else:
            self.dmas.append(op)
        return op

    def barrier(self):
        deps = list(self.last.values()) + self.dmas
        for e in self.ENGS:
            op = Op(e, None, list(deps), None)
            self.ops.append(op)
        self.dmas = []

    def emit(self):
        nc = self.nc
        engs = {"pe": nc.tensor, "act": nc.scalar, "dve": nc.vector, "pool": nc.gpsimd, "sp": nc.sync}
        esem = {e: nc.alloc_semaphore(name=f"sem_{e}") for e in ("pe", "act", "dve", "pool")}
        for op in self.ops:
            for d in op.deps:
                if d.dma_res is None:
                    if d.eng == "pe" and op.eng == "pe" and op.dma_res is None:
                        continue
                    d.needs_inc = True
        cnt = {e: 0 for e in esem}
        for op in self.ops:
            if op.dma_res is not None:
                R = op.dma_res
                if R.sem is None:
                    R.sem = nc.alloc_semaphore(name=f"dsem_{R.name}")
                R.ndma += 1
                op.tok = (R.sem, 16 * R.ndma)
            elif op.needs_inc:
                cnt[op.eng] += 1
                op.tok = (esem[op.eng], cnt[op.eng])
        per_eng = {e: [] for e in self.ENGS}
        last_dma = {}
        for op in self.ops:
            per_eng[op.eng].append(op)
            if op.dma_res is not None:
                last_dma[id(op.dma_res.sem)] = op.tok
        nw = 0
        for e in self.ENGS:
            eng = engs[e]
            waited = {}
            for op in per_eng[e]:
                need = {}
                for d in op.deps:
                    if d.tok is None:
                        continue
                    sem, val = d.tok
                    k = id(sem)
                    if waited.get(k, 0) >= val:
                        continue
                    if k not in need or need[k][1] < val:
                        need[k] = (sem, val)
                for k, (sem, val) in need.items():
                    eng.wait_ge(sem, val)
                    waited[k] = val
                    nw += 1
                if op.fn is None:
                    continue
                ins = op.fn(eng)
                if op.tok is not None:
                    ins.then_inc(op.tok[0], 16 if op.dma_res is not None else 1)
            if e == "sp":
                for k, (sem, val) in last_dma.items():
                    if waited.get(k, 0) < val:
                        eng.wait_ge(sem, val)
        return dict(n_ops=len(self.ops), n_waits=nw, cnt=cnt, n_dsem=len(last_dma))


class T:
    def __init__(self, ap, name):
        self.ap = ap
        self.res = Res(name)


def build(nc, NT, SEG=16, NL=2):
    NTOK = NT * 128
    NG = NT // 4
    P = Prog(nc)
    _n = [0]

    def sb(shape, dt, name=None):
        _n[0] += 1
        name = name or f"sb{_n[0]}"
        return T(nc.alloc_sbuf_tensor("s_" + name, list(shape), dt).ap(), name)

    def ps(shape, dt, name):
        t_ = T(nc.alloc_psum_tensor("p_" + name, list(shape), dt).ap(), name)
        t_.res.psum = True
        return t_

    def dram_in(name, shape, dt=F32):
        return nc.dram_tensor(name, list(shape), dt, kind="ExternalInput").ap()

    def dram_scr(name, shape, dt):
        return nc.dram_tensor(name, list(shape), dt, kind="Internal").ap()

    xin = dram_in("xin", [NTOK, D])
    ple = dram_in("ple", [NL, NTOK, 256])
    rot_d = dram_in("rot", [NTOK, NROT])
    link_d = dram_in("link", [128, 1])
    cst_d = dram_in("cst", [128, NCST])
    wmask_d = dram_in("wmask", [128, 17 * 128])
    prm_d = dram_in("prm", [NL, 128, NPRM])
    win_d = dram_in("w_in", [NL, 128, 8, NIN])
    wpc_d = dram_in("w_pc", [NL, NPIECE, 128, PW])
    yout = nc.dram_tensor("yout", [NTOK, D], F32, kind="ExternalOutput").ap()
    wbf = dram_scr("wbf", [NL, NPIECE, 128, PW], BF16)
    kT_s = dram_scr("kT_s", [NT, 128, 512], BF16)
    v_s = dram_scr("v_s", [NT, 128, 520], BF16)
    ob_s = dram_scr("ob_s", [NT, 128, 512], F32)
    mix_s = dram_scr("mix_s", [NT, 128, 1024], BF16)
    h1_s = dram_scr("h1_s", [NTOK, D], F32)

    cst = sb([128, NCST], F32, "cst")
    identb = sb([128, 128], BF16, "identb")
    Wm = sb([128, 17 * 128], BF16, "Wm")
    Wx = sb([128, 17 * 128], BF16, "Wx")
    prm = [sb([128, NPRM], F32, f"prm{l}") for l in range(NL)]
    linkc = sb([128, 1], F32, "link")
    arA = nc.alloc_sbuf_tensor("arA", [128, 8 * NIN], BF16).ap()
    arB = nc.alloc_sbuf_tensor("arB", [128, 19200], BF16).ap()
    arC = nc.alloc_sbuf_tensor("arC", [128, 8192], F32).ap()
    Win = T(arA.rearrange("p (k n) -> p k n", k=8), "Win")
    hid = T(arA[:, 0:16384].rearrange("p (f t) -> p f t", f=32), "hid")
    wring = [T(arA[:, 16384 + i * PW:16384 + (i + 1) * PW], f"wring{i}") for i in range(2)]
    KTr = [T(arB[:, i * 512:(i + 1) * 512].rearrange("p (c t) -> p c t", c=4), f"KT{i}") for i in range(RING)]
    Vr = [T(arB[:, 9216 + i * 520:9216 + (i + 1) * 520].rearrange("p (h e) -> p h e", h=8), f"V{i}") for i in range(RING)]
    hT = T(arB[:, 0:8192].bitcast(F32).rearrange("p (k t) -> p k t", k=8), "hT")
    xT = T(arB[:, 8192:12288].rearrange("p (k t) -> p k t", k=8), "xT")
    mixTg = T(arB[:, 12288:16384].rearrange("p (k t) -> p k t", k=8), "mixTg")
    mixTp = [T(mixTg.ap[:, :, i * 128:(i + 1) * 128], f"mixTp{i}") for i in range(4)]
    wst = [T(arC[:, i * 4096:(i + 1) * 4096], f"wst{i}") for i in range(2)]
    wring += [T(arC[:, i * 2048:(i + 1) * 2048].bitcast(BF16), f"wring{i + 2}") for i in range(2)]
    wcb = [T(arB[:, i * PW:(i + 1) * PW], f"wcb{i}") for i in range(2)]
    def cF(off, n):
        return arC[:, off:off + n]
    def cB(off, n):
        return arC[:, off:off + n // 2].bitcast(BF16)
    hld = [sb([128, D], F32, f"hld{i}") for i in range(2)]
    rotb = [sb([128, NROT], F32, f"rot{i}") for i in range(2)]
    obld = [T(cF(5120 + i * 512, 512), f"obld{i}") for i in range(2)]
    ub = sb([128, D], BF16, "ub")
    uT = [sb([128, 8, 128], BF16, f"uT{i}") for i in range(2)]
    ss1 = sb([128, 1], F32, "ss1")
    rs1 = sb([128, 1], F32, "rs1")
    junk = T(cF(0, 1024), "junk")
    zf = T(cF(1024, 512), "zf")
    zf_def = zf
    zsq = T(cF(1536, 512), "zsq")
    ss8 = sb([128, 8], F32, "ss8")
    rs8 = sb([128, 8], F32, "rs8")
    rA_def = T(cF(2048, 512), "rA")
    rB_def = T(cF(2560, 512), "rB")
    zfR = T(cF(3072, 512), "zfR")
    rAR = T(cF(3584, 512), "rAR")
    rBR = T(cF(4096, 512), "rBR")
    qkr = sb([128, 512], BF16, "qkr")
    QTd = [sb([128, 4, 256], BF16, f"QT{i}") for i in range(2)]
    KTst = [sb([128, 512], BF16, f"KTst{i}") for i in range(2)]
    Vst = [sb([128, 8, 65], BF16, f"Vst{i}") for i in range(2)]
    Pe = [[T(cB(3072 + (2 * i + j) * 256, 512).rearrange("p (c t) -> p c t", c=4), f"Pe{i}{j}") for j in range(2)] for i in range(2)]
    Pm = [[T(cB(4096 + (2 * i + j) * 256, 512).rearrange("p (c t) -> p c t", c=4), f"Pm{i}{j}") for j in range(2)] for i in range(2)]
    rden = sb([128, 8], F32, "rden")
    oa = sb([128, 8, 64], BF16, "oa")
    mixT = [sb([128, 8, 128], BF16, f"mixT{i}") for i in range(2)]
    bqk = sb([128, 256], F32, "bqk")
    bvb = sb([128, 256], BF16, "bvb")
    bg = sb([128, 32], F32, "bg")
    glrT = sb([33, 128], F32, "glrT")
    gex = sb([128, 128], F32, "gex")
    gsp = sb([128, 128], F32, "gsp")
    geq = sb([128, 128], F32, "geq")
    gek = sb([128, 128], F32, "gek")
    getot = sb([128, 1], F32, "getot")
    gqt = sb([128, 128], BF16, "gqt")
    gkt = sb([128, 128], BF16, "gkt")
    Qbd = sb([128, 4, 128], BF16, "Qbd")
    gqT = sb([128, 128], BF16, "gqT")
    gkT = sb([128, 128], BF16, "gkT")
    attm = sb([128, 4, 128], BF16, "attm")
    S32 = sb([128, 256], F32, "S32")
    Sbf = sb([128, 256], BF16, "Sbf")
    gtmp = sb([128, 256], F32, "gtmp")
    crot = sb([128, 512], BF16, "crot")
    cvb = sb([128, 256], BF16, "cvb")
    cvd = sb([128, 4, 64], BF16, "cvd")
    cqT = sb([64, 4, 128], BF16, "cqT")
    ckT = sb([64, 4, 128], BF16, "ckT")
    scD = sb([128, 4, 128], BF16, "scD")
    R32 = sb([64, 4, 64], F32, "R32")
    Rbf = sb([64, 4, 64], BF16, "Rbf")
    Dtab = [sb([128, 4, 128], F32, f"Dtab{d}") for d in range(2)]
    lgam = sb([128, 8], F32, "lgam")
    nlgam = sb([128, 8], F32, "nlgam")
    qdec = [sb([128, 4], F32, f"qdec{d}") for d in range(2)]
    kdec = [sb([128, 4], F32, f"kdec{d}") for d in range(2)]
    cdec = [sb([128, 4], F32, f"cdec{d}") for d in range(2)]
    rtmp = sb([128, 256], F32, "rtmp")
    obst = [sb([128, 512], F32, f"obst{i}") for i in range(2)]
    obc = sb([128, 512], F32, "obc")
    gx = T(cF(6144, 512), "gx")
    gs = T(cF(6656, 512), "gs")
    mixbc = sb([128, 512], BF16, "mixbc")
    gqs = sb([128, 64], F32, "gqs")
    pleld = [T(arA[:, 24576 + i * 512:24576 + (i + 1) * 512].bitcast(F32), f"pleld{i}") for i in range(2)]
    pleb = T(arA[:, 25600:25856], "pleb")
    pleT = T(arA[:, 25856:26880].rearrange("p (k t) -> p k t", k=2), "pleT")
    ost = [T(cF(4096 + i * 1024, 1024), f"ost{i}") for i in range(2)]
    sqb = [T(cF(6144 + i * 512, 512), f"sqb{i}") for i in range(2)]
    rstdT = T(cF(7168, 512), "rstdT")
    ones_f = sb([128, 128], F32, "ones_f")

    B0 = ps([128, 512], F32, "B0")
    B1 = ps([128, 512], F32, "B1")
    B2 = ps([128, 512], F32, "B2")
    B3 = ps([128, 512], F32, "B3")
    B4 = ps([128, 512], F32, "B4")
    B5 = ps([128, 512], F32, "B5")
    B6 = ps([128, 1024], BF16, "B6")
    B7 = ps([128, 512], F32, "B7")

    def R(*ts):
        return [t.res for t in ts]

    def mm(out, lhsT, rhs, reads, writes, start=True, stop=True, skip=False):
        P.add("pe", lambda e: e.matmul(out, lhsT=lhsT, rhs=rhs, start=start, stop=stop, skip_group_check=skip),
              reads=R(*reads), writes=R(*writes))

    def trp(out, in_, ident, reads, writes):
        P.add("pe", lambda e: e.transpose(out, in_, ident), reads=R(*reads), writes=R(*writes))

    def act(out, in_, func, reads, writes, scale=None, bias=None):
        kw = {}
        if scale is not None:
            kw["scale"] = scale
        if bias is not None:
            kw["bias"] = bias
        P.add("act", lambda e: e.activation(out=out, in_=in_, func=func, **kw), reads=R(*reads), writes=R(*writes))

    def cp(eng, out, in_, reads, writes):
        if eng == "act":
            P.add("act", lambda e: e.copy(out=out, in_=in_), reads=R(*reads), writes=R(*writes))
        else:
            P.add(eng, lambda e: e.tensor_copy(out=out, in_=in_), reads=R(*reads), writes=R(*writes))

    def tt(eng, out, in0, in1, op, reads, writes):
        P.add(eng, lambda e: e.tensor_tensor(out=out, in0=in0, in1=in1, op=op), reads=R(*reads), writes=R(*writes))

    def ts(eng, out, in0, s1, op0, reads, writes):
        P.add(eng, lambda e: e.tensor_scalar(out=out, in0=in0, scalar1=s1, scalar2=None, op0=op0),
              reads=R(*reads), writes=R(*writes))

    def stt(out, in0, scalar, in1, op0, op1, reads, writes):
        P.add("dve", lambda e: e.scalar_tensor_tensor(out=out, in0=in0, scalar=scalar, in1=in1, op0=op0, op1=op1),
              reads=R(*reads), writes=R(*writes))

    def ascale(out, in_, scale_ap, reads, writes):
        P.add("act", lambda e: e.activation(out=out, in_=in_, func=AF.Identity, scale=scale_ap), reads=R(*reads), writes=R(*writes))

    def red(out, in_, reads, writes):
        P.add("dve", lambda e: e.tensor_reduce(out=out, in_=in_, axis=AX.X, op=ALU.add), reads=R(*reads), writes=R(*writes))

    def rcp(out, in_, reads, writes):
        P.add("dve", lambda e: e.reciprocal(out=out, in_=in_), reads=R(*reads), writes=R(*writes))

    def dma(out, in_, reads, writes, sem):
        P.add("sp", lambda e: e.dma_start(out=out, in_=in_), reads=R(*reads), writes=R(*writes), dma=sem.res)

    def mset(eng, ap, val, writes):
        P.add(eng, lambda e: e.memset(ap, val), writes=R(*writes))

    def rstd_from_ss(out_t, ss_t, n, width):
        act(out_t.ap[:, 0:width], ss_t.ap[:, 0:width], AF.Ln, [ss_t], [out_t], scale=1.0 / n, bias=EPS)
        act(out_t.ap[:, 0:width], out_t.ap[:, 0:width], AF.Exp, [out_t], [out_t], scale=-0.5)

    def sigmoid_chain(out_ap, in_ap, tmp_ap, reads, tmp_t, out_t):
        act(tmp_ap, in_ap, AF.Exp, reads, [tmp_t], scale=-1.0)
        act(tmp_ap, tmp_ap, AF.Ln, [tmp_t], [tmp_t], bias=1.0)
        act(out_ap, tmp_ap, AF.Exp, [tmp_t], [out_t], scale=-1.0)

    c = cst.ap
    identf = c[:, K_ID:K_ID + 128]

    dma(cst.ap, cst_d, [], [cst], cst)
    dma(linkc.ap, link_d, [], [linkc], linkc)
    for l in range(NL):
        dma(prm[l].ap, prm_d[l], [], [prm[l]], prm[l])
    cp("dve", identb.ap, identf, [cst], [identb])
    mset("pool", ones_f.ap, 1.0, [ones_f])
    for i in range(2):
        dma(wst[i].ap[:, 0:1088], wmask_d[:, i * 1088:(i + 1) * 1088], [], [wst[i]], wst[i])
        cp("dve", Wm.ap[:, i * 1088:(i + 1) * 1088], wst[i].ap[:, 0:1088], [wst[i]], [Wm])
        ts("pool", Wx.ap[:, i * 1088:(i + 1) * 1088], wst[i].ap[:, 0:1088], linkc.ap[:, 0:1], ALU.mult, [wst[i], linkc], [Wx])
    mset("pool", glrT.ap[32:33, :], 1.0, [glrT])
    for i in range(2):
        mset("pool", QTd[i].ap, 0.0, [QTd[i]])
    for i in range(2):
        mset("pool", Vst[i].ap[:, :, 64:65], 1.0, [Vst[i]])

    cast_engs = ["dve", "act", "dve"]
    ce = 0
    for l in range(NL):
        for pc in range(NPIECE):
            s = (l * NPIECE + pc) % 2
            dma(wst[s].ap, wpc_d[l, pc], [], [wst[s]], wst[s])
            if 2 <= pc <= 9 or 18 <= pc <= 19:
                goff = P_LNMLP if pc <= 9 else P_LNPE
                for k in range(8):
                    if k % 2 == 0:
                        ts("dve", wcb[s].ap[:, k * 512:(k + 1) * 512], wst[s].ap[:, k * 512:(k + 1) * 512],
                           prm[l].ap[:, goff + k:goff + k + 1], ALU.mult, [wst[s], prm[l]], [wcb[s]])
                    else:
                        ascale(wcb[s].ap[:, k * 512:(k + 1) * 512], wst[s].ap[:, k * 512:(k + 1) * 512],
                               prm[l].ap[:, goff + k:goff + k + 1], [wst[s], prm[l]], [wcb[s]])
            else:
                for hh in range(2):
                    eng = cast_engs[ce % 3]
                    ce += 1
                    cp(eng, wcb[s].ap[:, hh * 2048:(hh + 1) * 2048], wst[s].ap[:, hh * 2048:(hh + 1) * 2048], [wst[s]], [wcb[s]])
            dma(wbf[l, pc], wcb[s].ap, [wcb[s]], [], wcb[s])
    P.barrier()

    def load_win(l):
        for k in range(8):
            s = k % 2
            dma(wst[s].ap[:, 0:NIN], win_d[l, :, k, :], [], [wst[s]], wst[s])
            half = NIN // 2
            ts("dve", Win.ap[:, k, 0:half], wst[s].ap[:, 0:half], prm[l].ap[:, P_LNMIX + k:P_LNMIX + k + 1], ALU.mult,
               [wst[s], prm[l]], [Win])
            ascale(Win.ap[:, k, half:NIN], wst[s].ap[:, half:NIN], prm[l].ap[:, P_LNMIX + k:P_LNMIX + k + 1],
                   [wst[s], prm[l]], [Win])

    def layer_tables(l):
        pr = prm[l]
        P.add("act", lambda e: e.mul(out=gqs.ap, in_=pr.ap[:, P_GQ:P_GQ + 64], mul=0.125), reads=R(pr), writes=R(gqs))
        act(lgam.ap, pr.ap[:, P_RAW:P_RAW + 8], AF.Exp, [pr], [lgam], scale=-1.0)
        act(lgam.ap, lgam.ap, AF.Ln, [lgam], [lgam], bias=1.0)
        P.add("act", lambda e: e.mul(out=nlgam.ap, in_=lgam.ap, mul=1.0), reads=R(lgam), writes=R(nlgam))
        P.add("act", lambda e: e.mul(out=lgam.ap, in_=lgam.ap, mul=-1.0), reads=R(lgam, nlgam), writes=R(lgam))
        for d in range(2):
            for h in range(4):
                lg = lgam.ap[:, d * 4 + h:d * 4 + h + 1]
                nlg = nlgam.ap[:, d * 4 + h:d * 4 + h + 1]
                act(Dtab[d].ap[:, h, :], c[:, K_TMS:K_TMS + 128], AF.Exp, [cst, lgam, nlgam], [Dtab[d]], scale=(lg if d == 0 else nlg))
                qsrc = c[:, K_TP1:K_TP1 + 1] if d == 0 else c[:, K_CT:K_CT + 1]
                ksrc = c[:, K_CM1:K_CM1 + 1] if d == 0 else c[:, K_S0:K_S0 + 1]
                act(qdec[d].ap[:, h:h + 1], qsrc, AF.Exp, [cst, lgam], [qdec[d]], scale=lg)
                act(kdec[d].ap[:, h:h + 1], ksrc, AF.Exp, [cst, lgam], [kdec[d]], scale=lg)
            for h in range(4):
                lg = lgam.ap[:, d * 4 + h:d * 4 + h + 1]
                P.add("act", lambda e, o=cdec[d].ap[:, h:h + 1], lg=lg: e.activation(out=o, in_=lg, func=AF.Exp, scale=128.0),
                      reads=R(lgam), writes=R(cdec[d]))
            mk = c[:, K_MF:K_MF + 128] if d == 0 else c[:, K_MB:K_MB + 128]
            tt("dve", Dtab[d].ap, Dtab[d].ap, mk.unsqueeze(1).to_broadcast([128, 4, 128]), ALU.mult, [Dtab[d], cst], [Dtab[d]])

    def front_load(t, slot, hsrc):
        h = hld[slot]
        dma(h.ap, hsrc[t * 128:(t + 1) * 128, :], [], [h], h)
        dma(rotb[slot].ap, rot_d[t * 128:(t + 1) * 128, :], [], [rotb[slot]], rotb[slot])

    def front(l, t, slot, hsrc):
        h = hld[slot]
        P.add("act", lambda e: e.activation(out=junk.ap, in_=h.ap, func=AF.Square, accum_out=ss1.ap),
              reads=R(h), writes=R(junk, ss1))
        rstd_from_ss(rs1, ss1, D, 1)
        ts("dve", ub.ap, h.ap, rs1.ap[:, 0:1], ALU.mult, [h, rs1], [ub])
        for k in range(8):
            trp(B6.ap[:, k * 128:(k + 1) * 128], ub.ap[:, k * 128:(k + 1) * 128], identb.ap, [ub, identb], [B6])
        cp("act", uT[slot].ap, B6.ap.rearrange("p (k t) -> p k t", k=8), [B6], [uT[slot]])

    def inproj(slot, c0, w, bank):
        for k in range(8):
            mm(bank.ap[:, 0:w], uT[slot].ap[:, k, :], Win.ap[:, k, c0:c0 + w], [uT[slot], Win], [bank], start=(k == 0), stop=(k == 7))

    def rotary(src, dst, half, cc, ns, rt, rA=None, rB=None):
        rA = rA or rA_def
        rB = rB or rB_def
        rd = 2 * half
        a3 = rA.ap.rearrange("p (h d) -> p h d", h=8)
        b3 = rB.ap.rearrange("p (h d) -> p h d", h=8)
        tt("dve", a3[:, :, 0:rd], src.ap.rearrange("p (h d) -> p h d", h=8)[:, :, 0:rd],
           cc.unsqueeze(1).to_broadcast([128, 8, rd]), ALU.mult, [src, rt], [rA])
        s3 = src.ap.rearrange("p (h d) -> p h d", h=8)
        tt("pool", b3[:, :, 0:half], s3[:, :, half:rd], ns[:, 0:half].unsqueeze(1).to_broadcast([128, 8, half]), ALU.mult, [src, rt], [rB])
        tt("dve", b3[:, :, half:rd], s3[:, :, 0:half], ns[:, half:rd].unsqueeze(1).to_broadcast([128, 8, half]), ALU.mult, [src, rt], [rB])
        d3 = dst.ap.rearrange("p (h d) -> p h d", h=8)
        tt("dve", d3[:, :, 0:rd], a3[:, :, 0:rd], b3[:, :, 0:rd], ALU.add, [rA, rB], [dst])
        if rd < 64:
            cp("act", d3[:, :, rd:64], s3[:, :, rd:64], [src], [dst])

    def qk_norm_rot(bank, gain_ap, gain_t, rt, dst):
        cp("act", zf.ap, bank.ap, [bank], [zf])
        act(zsq.ap, zf.ap, AF.Square, [zf], [zsq])
        red(ss8.ap, zsq.ap.rearrange("p (h d) -> p h d", h=8), [zsq], [ss8])
        rstd_from_ss(rs8, ss8, 64, 8)
        z3 = zf.ap.rearrange("p (h d) -> p h d", h=8)
        tt("dve", z3, z3, rs8.ap.unsqueeze(2).to_broadcast([128, 8, 64]), ALU.mult, [zf, rs8], [zf])
        tt("dve", z3, z3, gain_ap.unsqueeze(1).to_broadcast([128, 8, 64]), ALU.mult, [zf, gain_t], [zf])
        rotary(zf, dst, 8, rt.ap[:, 0:16], rt.ap[:, 16:32], rt)

    def c_rot(bank, rt, zf=None, rA=None, rB=None):
        zf = zf or zf_def
        cp("act", zf.ap[:, 0:256], bank.ap[:, 0:256], [bank], [zf])
        P.add("act", lambda e: e.mul(out=zf.ap[:, 256:512], in_=bank.ap[:, 256:512], mul=0.125), reads=R(bank), writes=R(zf))
        rotary(zf, crot, 32, rt.ap[:, 32:96], rt.ap[:, 96:160], rt, rA, rB)

    def state_link(d):
        ts("pool", S32.ap, S32.ap, linkc.ap[:, 0:1], ALU.mult, [S32, linkc], [S32])
        cp("act", Sbf.ap, S32.ap, [S32], [Sbf])
        ts("pool", R32.ap, R32.ap, linkc.ap[0:64, 0:1], ALU.mult, [R32, linkc], [R32])
        cp("act", Rbf.ap, R32.ap, [R32], [Rbf])

    _gl = [0]

    def GL():
        _gl[0] += 1
        lim = 10 ** 9 if STAGES is None else STAGES.get("glim", 10 ** 9)
        return _gl[0] <= lim

    def gla(l, d, tb_t=None, tb_ap=None):
        pr = prm[l]
        if tb_t is None:
            tb_t, tb_ap = B6, B6.ap[:, 512:768]
        _gl[0] = 0
        b1a = B1.ap[:, 0:256]
        if GL(): trp(B1.ap[0:32, 256:384], bg.ap, identf, [bg, cst], [B1])
        if GL(): cp("act", glrT.ap[0:32, :], B1.ap[0:32, 256:384], [B1], [glrT])
        if GL(): mm(B1.ap[:, 384:512], glrT.ap[0:33, :], pr.ap[0:33, P_GUP + d * 128:P_GUP + (d + 1) * 128], [glrT, pr], [B1])
        if GL(): act(gex.ap, B1.ap[:, 384:512], AF.Exp, [B1], [gex], scale=-1.0)
        if GL(): act(gsp.ap, gex.ap, AF.Ln, [gex], [gsp], bias=1.0)
        tri = c[:, K_TRII:K_TRII + 128] if d == 0 else c[:, K_TRIE:K_TRIE + 128]
        if GL(): mm(B1.ap[:, 256:384], tri, gsp.ap, [cst, gsp], [B1])
        if GL(): mm(B1.ap[:, 384:385], gsp.ap, c[:, K_NEG16:K_NEG16 + 1], [gsp, cst], [B1])
        sq_ = 1.0 if d == 0 else -1.0
        if GL(): act(geq.ap, B1.ap[:, 256:384], AF.Exp, [B1], [geq], scale=sq_)
        if GL(): act(gek.ap, B1.ap[:, 256:384], AF.Exp, [B1], [gek], scale=-sq_)
        if GL(): act(getot.ap, B1.ap[:, 384:385], AF.Exp, [B1], [getot])
        if GL(): stt(gqt.ap, bqk.ap[:, 0:128], 32.0 ** -0.5, geq.ap, ALU.mult, ALU.mult, [bqk, geq], [gqt])
        if GL(): tt("pool", gkt.ap, bqk.ap[:, 128:256], gek.ap, ALU.mult, [bqk, gek], [gkt])
        if GL(): trp(tb_ap[:, 0:128], gqt.ap, identb.ap, [gqt, identb], [tb_t])
        if GL(): trp(tb_ap[:, 128:256], gkt.ap, identb.ap, [gkt, identb], [tb_t])
        if GL(): tt("dve", Qbd.ap, tb_ap[:, 0:128].unsqueeze(1).to_broadcast([128, 4, 128]),
           c[:, K_BD:K_BD + 4].unsqueeze(2).to_broadcast([128, 4, 128]), ALU.mult, [tb_t, cst], [Qbd])
        if GL(): cp("act", gqT.ap, tb_ap[:, 0:128], [tb_t], [gqT])
        if GL(): cp("act", gkT.ap, tb_ap[:, 128:256], [tb_t], [gkT])
        if GL(): mm(B7.ap, gkT.ap, Qbd.ap.rearrange("p h t -> p (h t)"), [gkT, Qbd], [B7])
        mk = c[:, K_MF:K_MF + 128] if d == 0 else c[:, K_MB:K_MB + 128]
        if GL(): tt("dve", attm.ap, B7.ap.rearrange("p (h t) -> p h t", h=4), mk.unsqueeze(1).to_broadcast([128, 4, 128]), ALU.mult,
           [B7, cst], [attm])
        if d == 1:
            if GL(): ascale(S32.ap, S32.ap, getot.ap[:, 0:1], [S32, getot], [S32])
            if GL(): cp("act", Sbf.ap, S32.ap, [S32], [Sbf])
        if GL(): mm(b1a, gqT.ap, Sbf.ap, [gqT, Sbf], [B1], start=True, stop=False, skip=True)
        for h in range(4):
            if GL(): mm(B1.ap[:, 64 * h:64 * h + 64], attm.ap[:, h, :], bvb.ap[:, 64 * h:64 * h + 64], [attm, bvb], [B1],
               start=False, stop=(h == 3), skip=True)

    def gla_state(d, o_consumed_reads):
        mm(B1.ap[:, 256:512], gkt.ap, bvb.ap, [gkt, bvb], [B1])
        tt("dve", gtmp.ap.rearrange("p (h v) -> p h v", h=4), B1.ap[:, 256:512].rearrange("p (h v) -> p h v", h=4),
           c[:, K_BD:K_BD + 4].unsqueeze(2).to_broadcast([128, 4, 64]), ALU.mult, [B1, cst], [gtmp])
        if d == 0:
            ascale(S32.ap, S32.ap, getot.ap[:, 0:1], [S32, getot], [S32])
            stt(S32.ap, gtmp.ap, getot.ap[:, 0:1], S32.ap, ALU.mult, ALU.add, [gtmp, getot, S32], [S32])
            cp("act", Sbf.ap, S32.ap, [S32], [Sbf])
        else:
            tt("dve", S32.ap, S32.ap, gtmp.ap, ALU.add, [S32, gtmp], [S32])

    def ret(d, dst_ap, dst_t, Bo=None, Bs=None):
        Bo = Bo or B1
        Bs = Bs or B7
        for h in range(4):
            trp(B6.ap[0:64, h * 128:(h + 1) * 128], crot.ap[:, 64 * h:64 * h + 64], identb.ap, [crot, identb], [B6])
        for h in range(4):
            trp(B6.ap[0:64, 512 + h * 128:512 + (h + 1) * 128], crot.ap[:, 256 + 64 * h:256 + 64 * h + 64], identb.ap, [crot, identb], [B6])
        cp("act", cqT.ap, B6.ap[0:64, 0:512].rearrange("p (h t) -> p h t", h=4), [B6], [cqT])
        cp("dve", ckT.ap, B6.ap[0:64, 512:1024].rearrange("p (h t) -> p h t", h=4), [B6], [ckT])
        for h in range(4):
            mm(Bs.ap[:, h * 128:(h + 1) * 128], ckT.ap[:, h, :], cqT.ap[:, h, :], [ckT, cqT], [Bs])
        tt("dve", scD.ap, Bs.ap.rearrange("p (h t) -> p h t", h=4), Dtab[d].ap, ALU.mult, [Bs, Dtab[d]], [scD])
        tt("pool", cvd.ap, cvb.ap.rearrange("p (h e) -> p h e", h=4), kdec[d].ap.unsqueeze(2).to_broadcast([128, 4, 64]), ALU.mult,
           [cvb, kdec[d]], [cvd])
        for h in range(4):
            mm(Bo.ap[:, 64 * h:64 * h + 64], scD.ap[:, h, :], cvb.ap[:, 64 * h:64 * h + 64], [scD, cvb], [Bo])
        for h in range(4):
            mm(Bo.ap[:, 256 + 64 * h:256 + 64 * h + 64], cqT.ap[:, h, :], Rbf.ap[:, h, :], [cqT, Rbf], [Bo])
        tt("dve", rtmp.ap.rearrange("p (h e) -> p h e", h=4), Bo.ap[:, 256:512].rearrange("p (h e) -> p h e", h=4),
           qdec[d].ap.unsqueeze(2).to_broadcast([128, 4, 64]), ALU.mult, [Bo, qdec[d]], [rtmp])
        tt("dve", dst_ap, rtmp.ap, Bo.ap[:, 0:256], ALU.add, [rtmp, Bo], [dst_t])
        for h in range(4):
            mm(Bo.ap[0:64, 64 * h:64 * h + 64], crot.ap[:, 256 + 64 * h:256 + 64 * h + 64], cvd.ap[:, h, :], [crot, cvd], [Bo])
        tt("dve", R32.ap, R32.ap, cdec[d].ap[0:64, :].unsqueeze(2).to_broadcast([64, 4, 64]), ALU.mult, [R32, cdec[d]], [R32])
        tt("dve", R32.ap, R32.ap, Bo.ap[0:64, 0:256].rearrange("p (h e) -> p h e", h=4), ALU.add, [R32, Bo], [R32])
        cp("act", Rbf.ap, R32.ap, [R32], [Rbf])

    def reset_states():
        mset("pool", S32.ap, 0.0, [S32])
        mset("pool", Sbf.ap, 0.0, [Sbf])
        mset("pool", R32.ap, 0.0, [R32])
        mset("pool", Rbf.ap, 0.0, [Rbf])

    def pass1(l, hsrc):
        reset_states()
        order = list(range(NT - 1, -1, -1))
        front_load(order[0], 0, hsrc)
        for i, t in enumerate(order):
            slot = i % 2
            front(l, t, slot, hsrc)
            if i + 1 < NT:
                front_load(order[i + 1], (i + 1) % 2, hsrc)
            rt = rotb[slot]
            if t % SEG == SEG - 1 and t != NT - 1:
                state_link(1)
            ob = obst[slot]
            P.capture = []
            inproj(slot, C_AK, 512, B5)
            qk_norm_rot(B5, prm[l].ap[:, P_GK:P_GK + 64], prm[l], rt, qkr)
            b5b = B5.ap.bitcast(BF16)
            for cc_ in range(4):
                trp(b5b[:, cc_ * 128:(cc_ + 1) * 128], qkr.ap[:, cc_ * 128:(cc_ + 1) * 128], identb.ap, [qkr, identb], [B5])
            cp("dve", KTst[slot].ap, b5b[:, 0:512], [B5], [KTst[slot]])
            dma(kT_s[t], KTst[slot].ap, [KTst[slot]], [], KTst[slot])
            inproj(slot, C_AV, 512, B5)
            cp("act", Vst[slot].ap[:, :, 0:64], B5.ap.rearrange("p (h e) -> p h e", h=8), [B5], [Vst[slot]])
            dma(v_s[t], Vst[slot].ap.rearrange("p h e -> p (h e)"), [Vst[slot]], [], Vst[slot])
            chainKV = P.capture
            P.capture = []
            inproj(slot, C_B, 512, B0)
            cp("dve", bqk.ap, B0.ap[:, 0:256], [B0], [bqk])
            cp("act", bvb.ap, B0.ap[:, 256:512], [B0], [bvb])
            inproj(slot, C_G, 32, B0)
            cp("dve", bg.ap, B0.ap[:, 0:32], [B0], [bg])
            gla(l, 1, B0, B0.ap.bitcast(BF16)[:, 0:256])
            cp("act", ob.ap[:, 0:256], B1.ap[:, 0:256], [B1], [ob])
            gla_state(1, None)
            chainG = P.capture
            P.capture = []
            inproj(slot, C_CQK, 512, B2)
            c_rot(B2, rt, zfR, rAR, rBR)
            inproj(slot, C_CVG, 256, B2)
            cp("act", cvb.ap, B2.ap[:, 0:256], [B2], [cvb])
            ret(1, ob.ap[:, 256:512], ob, B3, B4)
            chainR = P.capture
            P.capture = None
            chains = [chainG, chainR, chainKV]
            pos_ = [0, 0, 0]
            while any(pos_[i_] < len(chains[i_]) for i_ in range(3)):
                for i_ in range(3):
                    if pos_[i_] < len(chains[i_]):
                        P.add(*chains[i_][pos_[i_]])
                        pos_[i_] += 1
            dma(ob_s[t], ob.ap, [ob], [], ob)

    def load_kv(kt):
        s = kt % RING
        dma(KTr[s].ap.rearrange("p c t -> p (c t)"), kT_s[kt], [], [KTr[s]], KTr[s])
        dma(Vr[s].ap.rearrange("p h e -> p (h e)"), v_s[kt], [], [Vr[s]], Vr[s])

    def pass2a(l, hsrc):
        reset_states()
        for kt in range(0, min(9, NT)):
            load_kv(kt)
        def q_front(t_, slot_):
            front(l, t_, slot_, hsrc)
            inproj(slot_, C_AQ, 512, B0)
            qk_norm_rot(B0, gqs.ap, gqs, rotb[slot_], qkr)
            for cc_ in range(4):
                trp(B6.ap[:, cc_ * 128:(cc_ + 1) * 128], qkr.ap[:, cc_ * 128:(cc_ + 1) * 128], identb.ap, [qkr, identb], [B6])
            cp("dve", QTd[slot_].ap[0:64, :, 0:128], B6.ap[0:64, 0:512].rearrange("p (c t) -> p c t", c=4), [B6], [QTd[slot_]])
            cp("act", QTd[slot_].ap[64:128, :, 128:256], B6.ap[64:128, 0:512].rearrange("p (c t) -> p c t", c=4), [B6], [QTd[slot_]])

        front_load(0, 0, hsrc)
        if NT > 1:
            front_load(1, 1, hsrc)
        dma(obld[0].ap, ob_s[0], [], [obld[0]], obld[0])
        q_front(0, 0)
        for t in range(NT):
            slot = t % 2
            QT = QTd[slot]
            if t + 9 < NT:
                load_kv(t + 9)
            if t + 1 < NT:
                dma(obld[(t + 1) % 2].ap, ob_s[t + 1], [], [obld[(t + 1) % 2]], obld[(t + 1) % 2])
            rt = rotb[slot]
            if t % SEG == 0 and t != 0:
                state_link(0)
            mt = mixT[slot]
            P.capture = []
            inproj(slot, C_B, 512, B0)
            cp("dve", bqk.ap, B0.ap[:, 0:256], [B0], [bqk])
            cp("act", bvb.ap, B0.ap[:, 256:512], [B0], [bvb])
            inproj(slot, C_G, 288, B0)
            cp("dve", bg.ap, B0.ap[:, 0:32], [B0], [bg])
            cp("act", gx.ap[:, 0:256], B0.ap[:, 32:288], [B0], [gx])
            gla(l, 0)
            tt("dve", obc.ap[:, 0:256], B1.ap[:, 0:256], obld[slot].ap[:, 0:256], ALU.add, [B1, obld[slot]], [obc])
            gla_state(0, None)
            inproj(slot, C_CQK, 512, B0)
            c_rot(B0, rt)
            inproj(slot, C_CVG, 512, B0)
            cp("act", cvb.ap, B0.ap[:, 0:256], [B0], [cvb])
            cp("dve", gx.ap[:, 256:512], B0.ap[:, 256:512], [B0], [gx])
            ret(0, rtmp.ap, rtmp)
            tt("dve", obc.ap[:, 256:512], rtmp.ap, obld[slot].ap[:, 256:512], ALU.add, [rtmp, obld[slot]], [obc])
            act(zsq.ap, obc.ap, AF.Square, [obc], [zsq])
            red(ss8.ap, zsq.ap.rearrange("p (h d) -> p h d", h=8), [zsq], [ss8])
            rstd_from_ss(rs8, ss8, 64, 8)
            o3 = obc.ap.rearrange("p (h d) -> p h d", h=8)
            tt("dve", o3, o3, rs8.ap.unsqueeze(2).to_broadcast([128, 8, 64]), ALU.mult, [obc, rs8], [obc])
            tt("dve", obc.ap, obc.ap, prm[l].ap[:, P_GOUT:P_GOUT + 512], ALU.mult, [obc, prm[l]], [obc])
            sigmoid_chain(gs.ap, gx.ap, gs.ap, [gx], gs, gs)
            tt("pool", gs.ap, gs.ap, gx.ap, ALU.mult, [gs, gx], [gs])
            tt("dve", mixbc.ap, obc.ap, gs.ap, ALU.mult, [obc, gs], [mixbc])
            for cc_ in range(4):
                trp(B6.ap[:, 512 + cc_ * 128:512 + (cc_ + 1) * 128], mixbc.ap[:, cc_ * 128:(cc_ + 1) * 128], identb.ap, [mixbc, identb], [B6])
            cp("act", mt.ap[:, 4:8, :], B6.ap[:, 512:1024].rearrange("p (c t) -> p c t", c=4), [B6], [mt])
            if t + 1 < NT:
                q_front(t + 1, (t + 1) % 2)
            side = P.capture
            P.capture = None
            side_pos = [0]

            def drip(n):
                while n > 0 and side_pos[0] < len(side):
                    P.add(*side[side_pos[0]])
                    side_pos[0] += 1
                    n -= 1

            def drain_side():
                drip(len(side))
            kts = list(range(max(0, t - 8), min(NT - 1, t + 8) + 1))
            first = [True, True]
            quota = -(-len(side) // len(kts))

            def pv(i, kt):
                s = kt % RING
                par = i % 2
                for b_, bank in ((0, B4), (1, B5)):
                    for bi in range(4):
                        hh = 4 * b_ + bi
                        mm(bank.ap[:, bi * 65:(bi + 1) * 65], Pm[par][b_].ap[:, bi, :], Vr[s].ap[:, hh, :],
                           [Pm[par][b_], Vr[s]], [bank], start=first[b_], stop=False, skip=True)
                        first[b_] = False

            for i, kt in enumerate(kts):
                s = kt % RING
                par = i % 2
                dlt = kt - t
                cross = (kt // SEG) != (t // SEG)
                mtab = Wx if cross else Wm
                mk = mtab.ap[:, (dlt + 8) * 128:(dlt + 9) * 128].unsqueeze(1).to_broadcast([128, 4, 128])
                for cc_ in range(4):
                    sbank = B2 if cc_ < 2 else B3
                    cl = cc_ % 2
                    mm(sbank.ap[:, cl * 256:(cl + 1) * 256], KTr[s].ap[:, cc_, :], QT.ap[:, cc_, :], [KTr[s], QT], [sbank])
                act(Pe[par][0].ap, B2.ap.rearrange("p (c t) -> p c t", c=4), AF.Exp, [B2], [Pe[par][0]])
                act(Pe[par][1].ap, B3.ap.rearrange("p (c t) -> p c t", c=4), AF.Exp, [B3], [Pe[par][1]])
                tt("dve", Pm[par][0].ap, Pe[par][0].ap, mk, ALU.mult, [Pe[par][0], mtab], [Pm[par][0]])
                tt("dve", Pm[par][1].ap, Pe[par][1].ap, mk, ALU.mult, [Pe[par][1], mtab], [Pm[par][1]])
                if i > 0:
                    pv(i - 1, kts[i - 1])
                drip(quota)
            pv(len(kts) - 1, kts[-1])
            for b_, bank in ((0, B4), (1, B5)):
                b3 = bank.ap[:, 0:260].rearrange("p (c e) -> p c e", c=4)
                rcp(rden.ap[:, b_ * 4:(b_ + 1) * 4], b3[:, :, 64], [bank], [rden])
                tt("dve", oa.ap[:, b_ * 4:(b_ + 1) * 4, :], b3[:, :, 0:64], rden.ap[:, b_ * 4:(b_ + 1) * 4].unsqueeze(2).to_broadcast([128, 4, 64]),
                   ALU.mult, [bank, rden], [oa])
            oaf = oa.ap.rearrange("p h d -> p (h d)")
            for cc_ in range(4):
                trp(B6.ap[:, cc_ * 128:(cc_ + 1) * 128], oaf[:, cc_ * 128:(cc_ + 1) * 128], identb.ap, [oa, identb], [B6])
            mt = mixT[slot]
            cp("act", mt.ap[:, 0:4, :], B6.ap[:, 0:512].rearrange("p (c t) -> p c t", c=4), [B6], [mt])
            drain_side()
            dma(mix_s[t], mt.ap.rearrange("p k t -> p (k t)"), [mt], [], mt)
            if t + 2 < NT:
                front_load(t + 2, slot, hsrc)

    def pass2b(l, hsrc, hdst):
        wctr = [0]

        def wload(pc):
            s = wctr[0] % 4
            wctr[0] += 1
            dma(wring[s].ap, wbf[l, pc], [], [wring[s]], wring[s])
            return wring[s]

        def norm_to_xT():
            for k in range(8):
                sq = sqb[k % 2]
                if k % 2 == 0:
                    tt("dve", sq.ap, hT.ap[:, k, :], hT.ap[:, k, :], ALU.mult, [hT], [sq])
                else:
                    act(sq.ap, hT.ap[:, k, :], AF.Square, [hT], [sq])
                mm(B4.ap, ones_f.ap, sq.ap, [ones_f, sq], [B4], start=(k == 0), stop=(k == 7))
            act(rstdT.ap, B4.ap, AF.Ln, [B4], [rstdT], scale=1.0 / D, bias=EPS)
            act(rstdT.ap, rstdT.ap, AF.Exp, [rstdT], [rstdT], scale=-0.5)
            for k in range(8):
                tt(["dve", "dve", "dve", "pool"][k % 4], xT.ap[:, k, :], hT.ap[:, k, :], rstdT.ap, ALU.mult, [hT, rstdT], [xT])

        for g in range(NG):
            for i in range(4):
                t = 4 * g + i
                slot = t % 2
                h = hld[slot]
                dma(h.ap, hsrc[t * 128:(t + 1) * 128, :], [], [h], h)
                dma(mixTp[i].ap, mix_s[t].rearrange("p (k t) -> p k t", k=8), [], [mixTp[i]], mixTp[i])
                pl = pleld[slot]
                dma(pl.ap, ple[l, t * 128:(t + 1) * 128, :], [], [pl], pl)
                for half in range(2):
                    bank = [B0, B1][half]
                    for kk in range(4):
                        k = half * 4 + kk
                        trp(bank.ap[:, kk * 128:(kk + 1) * 128], h.ap[:, k * 128:(k + 1) * 128], identf, [h, cst], [bank])
                    cp(["act", "dve"][half], hT.ap[:, half * 4:(half + 1) * 4, i * 128:(i + 1) * 128],
                       bank.ap.rearrange("p (k t) -> p k t", k=4), [bank], [hT])
                cp("dve", pleb.ap, pl.ap, [pl], [pleb])
                for k2 in range(2):
                    trp(B6.ap[:, k2 * 128:(k2 + 1) * 128], pleb.ap[:, k2 * 128:(k2 + 1) * 128], identb.ap, [pleb, identb], [B6])
                cp("act", pleT.ap[:, :, i * 128:(i + 1) * 128], B6.ap[:, 0:256].rearrange("p (k t) -> p k t", k=2), [B6], [pleT])
            for half in range(2):
                w = wload(0 + half)
                w3 = w.ap.rearrange("p (k n) -> p k n", k=8)
                for jj in range(4):
                    j = half * 4 + jj
                    bank = [B2, B3][j % 2]
                    for k in range(8):
                        mm(bank.ap, w3[:, k, jj * 128:(jj + 1) * 128], mixTg.ap[:, k, :], [w] + mixTp, [bank], start=(k == 0), stop=(k == 7))
                    tt(["dve", "pool"][0], hT.ap[:, j, :], hT.ap[:, j, :], bank.ap, ALU.add, [hT, bank], [hT])
            norm_to_xT()
            for fp in range(8):
                w = wload(2 + fp)
                w3 = w.ap.rearrange("p (k n) -> p k n", k=8)
                for ff in range(4):
                    f = fp * 4 + ff
                    bank = [B2, B3][f % 2]
                    for k in range(8):
                        mm(bank.ap, w3[:, k, ff * 128:(ff + 1) * 128], xT.ap[:, k, :], [w, xT], [bank], start=(k == 0), stop=(k == 7))
                    sq = sqb[f % 2]
                    act(sq.ap, bank.ap, AF.Relu, [bank], [sq])
                    tt("dve", hid.ap[:, f, :], sq.ap, bank.ap, ALU.mult, [sq, bank], [hid])
            for j in range(8):
                w = wload(10 + j)
                w3 = w.ap.rearrange("p (f n) -> p f n", f=32)
                bank = [B2, B3][j % 2]
                for f in range(32):
                    mm(bank.ap, w3[:, f, :], hid.ap[:, f, :], [w, hid], [bank], start=(f == 0), stop=(f == 31))
                tt("dve", hT.ap[:, j, :], hT.ap[:, j, :], bank.ap, ALU.add, [hT, bank], [hT])
            norm_to_xT()
            wp = wload(20)
            wp3 = wp.ap[:, 0:2048].rearrange("p (k n) -> p k n", k=2)
            for half in range(2):
                w = wload(18 + half)
                w3 = w.ap.rearrange("p (k n) -> p k n", k=8)
                for jj in range(4):
                    j = half * 4 + jj
                    bank = [B2, B3][j % 2]
                    for k in range(8):
                        mm(bank.ap, w3[:, k, jj * 128:(jj + 1) * 128], xT.ap[:, k, :], [w, xT], [bank], start=(k == 0), stop=(k == 7))
                    for k2 in range(2):
                        mm(B5.ap, wp3[:, k2, j * 128:(j + 1) * 128], pleT.ap[:, k2, :], [wp, pleT], [B5], start=(k2 == 0), stop=(k2 == 1))
                    sq = sqb[j % 2]
                    sigmoid_chain(sq.ap, bank.ap, sq.ap, [bank], sq, sq)
                    tt("dve", sq.ap, sq.ap, B5.ap, ALU.mult, [sq, B5], [sq])
                    tt("pool", hT.ap[:, j, :], hT.ap[:, j, :], sq.ap, ALU.add, [hT, sq], [hT])
            for i in range(4):
                t = 4 * g + i
                o = ost[t % 2]
                for half in range(2):
                    bank = [B0, B1][half]
                    for kk in range(4):
                        k = half * 4 + kk
                        trp(bank.ap[:, kk * 128:(kk + 1) * 128], hT.ap[:, k, i * 128:(i + 1) * 128], identf, [hT, cst], [bank])
                    cp(["act", "dve"][half], o.ap[:, half * 512:(half + 1) * 512], bank.ap, [bank], [o])
                dma(hdst[t * 128:(t + 1) * 128, :], o.ap, [o], [], o)

    for l in range(NL):
        hsrc = xin if l == 0 else h1_s
        hdst = h1_s if l == 0 else yout
        if STAGES is not None and l >= STAGES.get("nl", 2):
            break
        load_win(l)
        layer_tables(l)
        P.barrier()
        if STAGES is None or "p1" in STAGES:
            pass1(l, hsrc)
        P.barrier()
        if STAGES is None or "p2a" in STAGES:
            pass2a(l, hsrc)
        P.barrier()
        if STAGES is None or "p2b" in STAGES:
            pass2b(l, hsrc, hdst)
        P.barrier()
    if STAGES is not None:
        dma(hld[0].ap, xin[0:128, :], [], [hld[0]], hld[0])
        dma(yout[0:128, :], hld[0].ap, [hld[0]], [], hld[0])
    return P.emit()


def _consts():
    s = np.arange(128)[:, None]
    t = np.arange(128)[None, :]
    c = np.zeros((128, NCST), np.float32)
    c[:, K_ID:K_ID + 128] = np.eye(128)
    c[:, K_TRII:K_TRII + 128] = np.where(s <= t, -1.0 / 16, 0.0)
    c[:, K_TRIE:K_TRIE + 128] = np.where(s < t, -1.0 / 16, 0.0)
    c[:, K_MF:K_MF + 128] = (s <= t)
    c[:, K_MB:K_MB + 128] = (s > t)
    c[:, K_TMS:K_TMS + 128] = (t - s)
    p = np.arange(128)
    c[:, K_TP1] = p + 1
    c[:, K_CM1] = 127 - p
    c[:, K_CT] = 128 - p
    c[:, K_S0] = p
    c[:, K_NEG16] = -1.0 / 16
    c[:, K_ONE] = 1.0
    for h in range(4):
        c[:, K_BD + h] = (p // 32 == h)
    wm = np.zeros((128, 17 * 128), np.float32)
    for dl in range(-8, 9):
        off = dl * 128 + s - t
        a = np.abs(off)
        w = (a <= 64).astype(np.float32) + ((off % 4 == 0) & (a <= 256)) + ((off % 16 == 0) & (a <= 1024))
        wm[:, (dl + 8) * 128:(dl + 9) * 128] = w
    return c, wm


def _rot_table(pos):
    out = np.zeros((pos.shape[0], NROT), np.float32)

    def tab(rot_dim, theta):
        half = rot_dim // 2
        inv = (np.float32(1.0) / (np.float32(theta) ** (np.arange(half, dtype=np.float32) * np.float32(2.0 / rot_dim)))).astype(np.float32)
        ang = (pos[:, None].astype(np.float32) * inv[None, :]).astype(np.float32)
        cs = np.cos(ang.astype(np.float64)).astype(np.float32)
        sn = np.sin(ang.astype(np.float64)).astype(np.float32)
        return np.concatenate([cs, cs], 1), np.concatenate([-sn, sn], 1)

    cc, ns = tab(16, 500000.0)
    out[:, 0:16], out[:, 16:32] = cc, ns
    cc, ns = tab(64, 10000.0)
    out[:, 32:96], out[:, 96:160] = cc, ns
    return out


def _weights(inp, NL=2):
    f = np.float32
    w_in = np.asarray(inp["w_in"], f)
    perm = np.concatenate([np.arange(0, 2048), np.arange(2304, 2336), np.arange(2048, 2304), np.arange(2336, 3360)])
    win = np.ascontiguousarray(w_in[:, :, perm].reshape(NL, 8, 128, NIN).transpose(0, 2, 1, 3))
    pcs = np.zeros((NL, NPIECE, 128, PW), f)
    for l in range(NL):
        wo = np.asarray(inp["w_out"][l], f).reshape(8, 128, 1024).transpose(1, 0, 2)
        for hh in range(2):
            pcs[l, hh] = wo[:, :, hh * 512:(hh + 1) * 512].reshape(128, PW)
        w1 = np.asarray(inp["w_mlp_in"][l], f).reshape(8, 128, 4096).transpose(1, 0, 2)
        for fp in range(8):
            pcs[l, 2 + fp] = w1[:, :, fp * 512:(fp + 1) * 512].reshape(128, PW)
        w2 = np.asarray(inp["w_mlp_out"][l], f).reshape(32, 128, 1024).transpose(1, 0, 2)
        for j in range(8):
            pcs[l, 10 + j] = w2[:, :, j * 128:(j + 1) * 128].reshape(128, PW)
        wg = np.asarray(inp["w_pe_gate"][l], f).reshape(8, 128, 1024).transpose(1, 0, 2)
        for hh in range(2):
            pcs[l, 18 + hh] = wg[:, :, hh * 512:(hh + 1) * 512].reshape(128, PW)
        wp = np.asarray(inp["w_pe_proj"][l], f).reshape(2, 128, 1024).transpose(1, 0, 2)
        pcs[l, 20, :, 0:2048] = wp.reshape(128, 2048)
    prm = np.zeros((NL, 128, NPRM), f)
    for l in range(NL):
        prm[l, :, P_LNMIX:P_LNMIX + 8] = np.asarray(inp["ln_mix"][l], f).reshape(8, 128).T
        prm[l, :, P_LNMLP:P_LNMLP + 8] = np.asarray(inp["ln_mlp"][l], f).reshape(8, 128).T
        prm[l, :, P_LNPE:P_LNPE + 8] = np.asarray(inp["ln_pe"][l], f).reshape(8, 128).T
        prm[l, :, P_GQ:P_GQ + 64] = np.asarray(inp["attn_q_norm"][l], f)[None]
        prm[l, :, P_GK:P_GK + 64] = np.asarray(inp["attn_k_norm"][l], f)[None]
        prm[l, :, P_GOUT:P_GOUT + 256] = np.asarray(inp["gla_out_norm"][l], f)[None]
        prm[l, :, P_GOUT + 256:P_GOUT + 512] = np.asarray(inp["ret_out_norm"][l], f)[None]
        prm[l, :, P_RAW:P_RAW + 8] = np.asarray(inp["ret_decay_raw"][l], f).reshape(8)[None]
        gu = np.asarray(inp["gla_gate_up"][l], f)
        gb = np.asarray(inp["gla_gate_bias"][l], f)
        prm[l, 0:16, P_GUP:P_GUP + 128] = gu[0]
        prm[l, 16:32, P_GUP + 128:P_GUP + 256] = gu[1]
        prm[l, 32, P_GUP:P_GUP + 128] = gb[0]
        prm[l, 32, P_GUP + 128:P_GUP + 256] = gb[1]
    return win, pcs, prm


_CACHE = {}


def _program(NT):
    if NT not in _CACHE:
        nc = bass.Bass("TRN2", target_bir_lowering=False)
        stats = build(nc, NT)
        _CACHE[NT] = (nc, stats)
    return _CACHE[NT][0]


def run_cores(core_specs, inp, NT):
    cst, wm = _consts()
    win, pcs, prm = _weights(inp)
    in_maps = []
    for cs in core_specs:
        in_maps.append({
            "xin": np.ascontiguousarray(cs["x"], np.float32),
            "ple": np.ascontiguousarray(cs["ple"], np.float32),
            "rot": _rot_table(cs["pos"].astype(np.float32)),
            "link": np.full((128, 1), cs["link"], np.float32),
            "cst": cst, "wmask": wm, "prm": prm, "w_in": win, "w_pc": pcs,
        })
    nc = _program(NT)
    res = run_bass_kernel_spmd(nc, in_maps, core_ids=list(range(len(in_maps))))
    return [r["yout"] for r in res.results]


def kernel(**inp):
    xp = np.asarray(inp["x_prompt"], np.float32)
    xs = np.asarray(inp["x_sample"], np.float32)
    pp = np.asarray(inp["p_prompt"], np.float32)
    psm = np.asarray(inp["p_sample"], np.float32)
    NT = 128
    NTOK = NT * 128
    SL = 2048
    groups = [list(range(0, 6)), list(range(6, 12)), list(range(12, 17)), list(range(17, 22)), list(range(22, 27)), list(range(27, 32))]
    specs = []
    for b in range(2):
        specs.append(dict(x=xp[b], ple=pp[:, b], pos=np.arange(NTOK), link=1.0))
    for gsq in groups:
        x = np.zeros((NTOK, D), np.float32)
        pl = np.zeros((2, NTOK, 256), np.float32)
        for i, sq in enumerate(gsq):
            x[i * SL:(i + 1) * SL] = xs[sq]
            pl[:, i * SL:(i + 1) * SL] = psm[:, sq]
        specs.append(dict(x=x, ple=pl, pos=np.tile(np.arange(SL), NTOK // SL), link=0.0))
    outs = run_cores(specs, inp, NT)
    y_prompt = np.stack([outs[0], outs[1]], 0).astype(np.float32)
    y_sample = np.zeros_like(xs)
    for ci, gsq in enumerate(groups):
        for i, sq in enumerate(gsq):
            y_sample[sq] = outs[2 + ci][i * SL:(i + 1) * SL]
    return (y_prompt, y_sample)
```
